# Optimizing a Trainium2 kernel written in Bass

```python
import jax, jax.numpy as jnp
from jax import lax
import numpy as np

D_MODEL = 1024
BATCH = 8
SEQ = 2048
DEPTH = 2
DEC_BATCH = 128
DEC_SEQ = 4
PAST_LEN = 16384
PAGE_SIZE = 128

BRANCH_W = D_MODEL // 2
N_BRANCH = 3
LRU_W = BRANCH_W
LRU_BLOCKS = 8
LRU_BW = LRU_W // LRU_BLOCKS
CONV_W = 4
LRU_C = 8.0
HG_HEADS = 4
HG_DK = BRANCH_W // HG_HEADS
HG_DV = BRANCH_W // HG_HEADS
HG_CHUNK = 64
HG_F_MIN = 1e-20
RW_HD = 64
RW_HEADS = BRANCH_W // RW_HD
RW_LORA_W = 64
RW_LORA_A = 64
RW_LORA_G = 128
RW_GN_EPS = 64e-5
RW_COLS = 3 * BRANCH_W + RW_LORA_W + RW_LORA_A + RW_LORA_G
D_FF = ((8 * D_MODEL // 3 + 255) // 256) * 256
PLE_DIM = 256
EPS = 1e-6
IN_SPLIT = (LRU_W, LRU_W, HG_HEADS * HG_DK, HG_HEADS * HG_DK, HG_HEADS * HG_DV, HG_HEADS * HG_DV, RW_COLS, N_BRANCH * D_MODEL)
IN_COLS = sum(IN_SPLIT)

kernel_name = 'hybrid_rglru_hgrn2_rwkv7_step'


def _f32(t):
    return t.astype(jnp.float32)


def _split(t, sizes):
    idx = [int(i) for i in np.cumsum(sizes)[:-1]]
    return jnp.split(t, idx, axis=-1)


def rmsnorm(x, g):
    xf = _f32(x)
    y = xf * lax.rsqrt(jnp.mean(xf * xf, axis=-1, keepdims=True) + EPS)
    return (y * _f32(g)).astype(x.dtype)


def rglru_branch(xa, ga, conv_state, h0, pos, conv_w, conv_b, wa, ba, wx, bx, lam):
    B, S, W = xa.shape
    xcat = jnp.concatenate([_f32(conv_state), _f32(xa)], axis=1)
    cw = _f32(conv_w)
    xc = _f32(conv_b) + xcat[:, 0:S] * cw[0]
    for j in range(1, CONV_W):
        xc = xc + xcat[:, j:j + S] * cw[j]
    new_conv = xcat[:, S:]
    xb = xc.reshape(B, S, LRU_BLOCKS, LRU_BW)
    r = jax.nn.sigmoid(jnp.einsum('bshi,hij->bshj', xb, _f32(wa)).reshape(B, S, W) + _f32(ba))
    i = jax.nn.sigmoid(jnp.einsum('bshi,hij->bshj', xb, _f32(wx)).reshape(B, S, W) + _f32(bx))
    log_a = -LRU_C * r * jax.nn.softplus(-_f32(lam))
    a = jnp.exp(log_a)
    mult = jnp.where((pos == 0)[None, :, None], 1.0, jnp.sqrt(jnp.maximum(-jnp.expm1(2.0 * log_a), 0.0)))
    b = xc * i * mult
    b = b.at[:, 0].add(a[:, 0] * _f32(h0))

    def comb(e, l):
        return (e[0] * l[0], l[0] * e[1] + l[1])

    _, h = lax.associative_scan(comb, (a, b), axis=1)
    y = h * jax.nn.gelu(_f32(ga), approximate=True)
    return y, new_conv, h[:, -1]


def hgrn2_branch(q, f_pre, v, g, s0, lb, norm_g):
    B, S, _ = q.shape
    C = HG_CHUNK if S % HG_CHUNK == 0 else S
    N = S // C
    q = jax.nn.silu(_f32(q))
    lb = _f32(lb)
    fp = _f32(f_pre)
    f = lb + (1.0 - lb) * jax.nn.sigmoid(fp)
    k = (1.0 - lb) * jax.nn.sigmoid(-fp)
    logf = jnp.log(jnp.maximum(f, HG_F_MIN))

    def chunks(t, d):
        return jnp.moveaxis(t.reshape(B, N, C, HG_HEADS, d), 1, 0)

    xs = (chunks(q, HG_DK), chunks(k, HG_DK), chunks(_f32(v), HG_DV), chunks(logf, HG_DK))
    causal = jnp.tril(jnp.ones((C, C), bool))

    def step(st, inp):
        qc, kc, vc, lc = inp
        bc = jnp.cumsum(lc, axis=1)
        diff = bc[:, :, None] - bc[:, None, :]
        dec = jnp.where(causal[None, :, :, None, None], jnp.exp(jnp.minimum(diff, 0.0)), 0.0)
        A = jnp.einsum('bthk,bshk,btshk->bhts', qc, kc, dec)
        o = jnp.einsum('bhts,bshv->bthv', A, vc) + jnp.einsum('bthk,bhkv->bthv', qc * jnp.exp(bc), st)
        btot = bc[:, -1]
        st = jnp.exp(btot)[..., None] * st + jnp.einsum('bshk,bshv->bhkv', kc * jnp.exp(btot[:, None] - bc), vc)
        return st, o

    s_new, o = lax.scan(step, _f32(s0), xs)
    o = jnp.moveaxis(o, 0, 1).reshape(B, S, HG_HEADS, HG_DV)
    o = o * lax.rsqrt(jnp.mean(o * o, axis=-1, keepdims=True) + EPS) * _f32(norm_g).reshape(HG_HEADS, HG_DV)
    y = o.reshape(B, S, HG_HEADS * HG_DV) * jax.nn.silu(_f32(g))
    return y, s_new


def rwkv7_branch(cblk, shift0, s0, mu, w0, w_up, a0, a_up, g_up, k_k, k_a, r_k, ln_g, ln_b):
    B, S, _ = cblk.shape
    c = _f32(cblk)
    prev = jnp.concatenate([_f32(shift0)[:, None], c[:, :-1]], axis=1)
    xm = c + (prev - c) * _f32(mu)
    new_shift = c[:, -1]
    r, k, v, wl, al, gl = _split(xm, (BRANCH_W, BRANCH_W, BRANCH_W, RW_LORA_W, RW_LORA_A, RW_LORA_G))
    w = -jax.nn.softplus(-(_f32(w0) + jnp.tanh(wl) @ _f32(w_up))) - 0.5
    decay = jnp.exp(-jnp.exp(w))
    a = jax.nn.sigmoid(_f32(a0) + al @ _f32(a_up))
    g = jax.nn.sigmoid(gl) @ _f32(g_up)

    def heads(t):
        return t.reshape(B, S, RW_HEADS, RW_HD)

    kk = heads(k * _f32(k_k))
    kk = kk / jnp.maximum(jnp.sqrt(jnp.sum(kk * kk, axis=-1, keepdims=True)), 1e-12)
    k = k * (1.0 + (a - 1.0) * _f32(k_a))
    rh, kh, vh, ah, dh = heads(r), heads(k), heads(v), heads(a), heads(decay)

    def step(st, inp):
        rt, wt, kt, vt, kkt, at = inp
        sk = jnp.einsum('bhvk,bhk->bhv', st, kkt)
        st = st * wt[:, :, None, :] - sk[..., None] * (kkt * at)[:, :, None, :] + vt[..., None] * kt[:, :, None, :]
        o = jnp.einsum('bhvk,bhk->bhv', st, rt)
        return st, o

    xs = tuple(jnp.moveaxis(t, 1, 0) for t in (rh, dh, kh, vh, kk, ah))
    s_new, o = lax.scan(step, _f32(s0), xs)
    o = jnp.moveaxis(o, 0, 1)
    mean = jnp.mean(o, axis=-1, keepdims=True)
    var = jnp.mean((o - mean) ** 2, axis=-1, keepdims=True)
    o = (o - mean) * lax.rsqrt(var + RW_GN_EPS) * _f32(ln_g).reshape(RW_HEADS, RW_HD) + _f32(ln_b).reshape(RW_HEADS, RW_HD)
    o = o + jnp.sum(rh * kh * _f32(r_k), axis=-1, keepdims=True) * vh
    y = o.reshape(B, S, BRANCH_W) * g
    return y, s_new, new_shift


def decoder_layer(x, p, st, pos, lb, L):
    conv0, lru0, hg0, rw0, sh0 = st
    B, S, _ = x.shape
    dt = x.dtype
    h = rmsnorm(x, L['norm_pre_mix'])
    proj = jnp.einsum('bsd,dc->bsc', h, L['w_in'])
    xa, ga, bq, bf, bi, bg, cblk, gts = _split(proj, IN_SPLIT)
    yA, nconv, nlru = rglru_branch(xa, ga, conv0, lru0, pos, L['conv_w'], L['conv_b'], L['lru_wa'], L['lru_ba'], L['lru_wx'], L['lru_bx'], L['lru_lambda'])
    yB, nhg = hgrn2_branch(bq, bf, bi, bg, hg0, lb, L['hg_norm_g'])
    yC, nrw, nsh = rwkv7_branch(cblk, sh0, rw0, L['rw_mu'], L['rw_w0'], L['rw_w_up'], L['rw_a0'], L['rw_a_up'], L['rw_g_up'], L['rw_k_k'], L['rw_k_a'], L['rw_r_k'], L['rw_ln_g'], L['rw_ln_b'])
    br = jnp.stack([yA, yB, yC], axis=2).astype(dt)
    up = jnp.einsum('bsnw,nwd->bsnd', br, L['w_branch'])
    gates = jax.nn.sigmoid(gts.reshape(B, S, N_BRANCH, D_MODEL))
    mix = jnp.einsum('bsd,de->bse', jnp.sum(gates * up, axis=2), L['w_out'])
    x = x + rmsnorm(mix, L['norm_post_mix'])
    h = rmsnorm(x, L['norm_pre_ffn'])
    ff = (jax.nn.silu(h @ L['w_ffn_gate']) * (h @ L['w_ffn_up'])) @ L['w_ffn_down']
    x = x + rmsnorm(ff, L['norm_post_ffn'])
    ple = (p @ L['w_ple']) * jax.nn.sigmoid(x @ L['w_ple_gate'])
    x = x + rmsnorm(ple, L['norm_ple'])
    return x, (nconv.astype(conv0.dtype), nlru.astype(lru0.dtype), nhg.astype(hg0.dtype), nrw.astype(rw0.dtype), nsh.astype(sh0.dtype))


def run_trunk(x, p, states, pos, lb_all, W):
    new = ([], [], [], [], [])
    for i in range(DEPTH):
        Li = {name: w[i] for name, w in W.items()}
        x, ns = decoder_layer(x, p[i], tuple(s[i] for s in states), pos, lb_all[i], Li)
        for lst, t in zip(new, ns):
            lst.append(t)
    return x, tuple(jnp.stack(l) for l in new)


def setup_inputs(seed: int = 0) -> dict:
    key = jax.random.key(seed)
    ks = iter(jax.random.split(key, 64))
    f32 = jnp.float32

    def nrm(shape, scale):
        return scale * jax.random.normal(next(ks), shape, f32)

    def gain(shape):
        return 1.0 + 0.05 * jax.random.normal(next(ks), shape, f32)

    def unif(shape, lo, hi):
        return jax.random.uniform(next(ks), shape, f32, lo, hi)

    u = unif((DEPTH, LRU_W), 0.9, 0.999)
    s = u ** (1.0 / LRU_C)
    lam = jnp.log(s) - jnp.log1p(-s)
    return {
        'x_prompt': nrm((BATCH, SEQ, D_MODEL), 1.0),
        'x_sample': nrm((DEC_BATCH, DEC_SEQ, D_MODEL), 1.0),
        'p_prompt': nrm((DEPTH, BATCH, SEQ, PLE_DIM), 1.0),
        'p_sample': nrm((DEPTH, DEC_BATCH, DEC_SEQ, PLE_DIM), 1.0),
        'state_conv_a': nrm((DEPTH, DEC_BATCH, CONV_W - 1, LRU_W), 1.0),
        'state_lru_a': nrm((DEPTH, DEC_BATCH, LRU_W), 0.5),
        'state_hgrn': nrm((DEPTH, DEC_BATCH, HG_HEADS, HG_DK, HG_DV), 0.5),
        'state_rwkv': nrm((DEPTH, DEC_BATCH, RW_HEADS, RW_HD, RW_HD), 0.3),
        'state_shift_c': nrm((DEPTH, DEC_BATCH, RW_COLS), 1.0),
        'norm_pre_mix': gain((DEPTH, D_MODEL)),
        'w_in': nrm((DEPTH, D_MODEL, IN_COLS), D_MODEL ** -0.5),
        'conv_w': nrm((DEPTH, CONV_W, LRU_W), CONV_W ** -0.5),
        'conv_b': nrm((DEPTH, LRU_W), 0.02),
        'lru_wa': nrm((DEPTH, LRU_BLOCKS, LRU_BW, LRU_BW), LRU_BW ** -0.5),
        'lru_ba': nrm((DEPTH, LRU_W), 0.02),
        'lru_wx': nrm((DEPTH, LRU_BLOCKS, LRU_BW, LRU_BW), LRU_BW ** -0.5),
        'lru_bx': nrm((DEPTH, LRU_W), 0.02),
        'lru_lambda': lam,
        'hg_lower_bounds': nrm((DEPTH, HG_HEADS * HG_DK), 1.0),
        'hg_norm_g': gain((DEPTH, HG_HEADS * HG_DV)),
        'rw_mu': unif((DEPTH, RW_COLS), 0.0, 1.0),
        'rw_w0': unif((DEPTH, BRANCH_W), -6.0, 1.0),
        'rw_w_up': nrm((DEPTH, RW_LORA_W, BRANCH_W), 0.5 * RW_LORA_W ** -0.5),
        'rw_a0': nrm((DEPTH, BRANCH_W), 0.5),
        'rw_a_up': nrm((DEPTH, RW_LORA_A, BRANCH_W), RW_LORA_A ** -0.5),
        'rw_g_up': nrm((DEPTH, RW_LORA_G, BRANCH_W), RW_LORA_G ** -0.5),
        'rw_k_k': 0.85 + nrm((DEPTH, BRANCH_W), 0.05),
        'rw_k_a': gain((DEPTH, BRANCH_W)),
        'rw_r_k': nrm((DEPTH, RW_HEADS, RW_HD), 0.1),
        'rw_ln_g': gain((DEPTH, BRANCH_W)),
        'rw_ln_b': nrm((DEPTH, BRANCH_W), 0.02),
        'w_branch': nrm((DEPTH, N_BRANCH, BRANCH_W, D_MODEL), BRANCH_W ** -0.5),
        'w_out': nrm((DEPTH, D_MODEL, D_MODEL), D_MODEL ** -0.5),
        'norm_post_mix': gain((DEPTH, D_MODEL)),
        'norm_pre_ffn': gain((DEPTH, D_MODEL)),
        'w_ffn_gate': nrm((DEPTH, D_MODEL, D_FF), D_MODEL ** -0.5),
        'w_ffn_up': nrm((DEPTH, D_MODEL, D_FF), D_MODEL ** -0.5),
        'w_ffn_down': nrm((DEPTH, D_FF, D_MODEL), D_FF ** -0.5),
        'norm_post_ffn': gain((DEPTH, D_MODEL)),
        'w_ple': nrm((DEPTH, PLE_DIM, D_MODEL), PLE_DIM ** -0.5),
        'w_ple_gate': nrm((DEPTH, D_MODEL, D_MODEL), D_MODEL ** -0.5),
        'norm_ple': gain((DEPTH, D_MODEL)),
    }


def reference(x_prompt, x_sample, p_prompt, p_sample, state_conv_a, state_lru_a, state_hgrn, state_rwkv, state_shift_c,
              norm_pre_mix, w_in, conv_w, conv_b, lru_wa, lru_ba, lru_wx, lru_bx, lru_lambda, hg_lower_bounds, hg_norm_g,
              rw_mu, rw_w0, rw_w_up, rw_a0, rw_a_up, rw_g_up, rw_k_k, rw_k_a, rw_r_k, rw_ln_g, rw_ln_b,
              w_branch, w_out, norm_post_mix, norm_pre_ffn, w_ffn_gate, w_ffn_up, w_ffn_down, norm_post_ffn,
              w_ple, w_ple_gate, norm_ple):
    W = dict(norm_pre_mix=norm_pre_mix, w_in=w_in, conv_w=conv_w, conv_b=conv_b, lru_wa=lru_wa, lru_ba=lru_ba,
             lru_wx=lru_wx, lru_bx=lru_bx, lru_lambda=lru_lambda, hg_norm_g=hg_norm_g, rw_mu=rw_mu, rw_w0=rw_w0,
             rw_w_up=rw_w_up, rw_a0=rw_a0, rw_a_up=rw_a_up, rw_g_up=rw_g_up, rw_k_k=rw_k_k, rw_k_a=rw_k_a,
             rw_r_k=rw_r_k, rw_ln_g=rw_ln_g, rw_ln_b=rw_ln_b, w_branch=w_branch, w_out=w_out,
             norm_post_mix=norm_post_mix, norm_pre_ffn=norm_pre_ffn, w_ffn_gate=w_ffn_gate, w_ffn_up=w_ffn_up,
             w_ffn_down=w_ffn_down, norm_post_ffn=norm_post_ffn, w_ple=w_ple, w_ple_gate=w_ple_gate, norm_ple=norm_ple)
    sm = jax.nn.softmax(_f32(hg_lower_bounds), axis=0)
    lb_all = jnp.cumsum(sm, axis=0) - sm[0]
    dt = x_prompt.dtype
    bp = x_prompt.shape[0]
    zero_states = (jnp.zeros((DEPTH, bp, CONV_W - 1, LRU_W), dt), jnp.zeros((DEPTH, bp, LRU_W), dt),
                   jnp.zeros((DEPTH, bp, HG_HEADS, HG_DK, HG_DV), dt), jnp.zeros((DEPTH, bp, RW_HEADS, RW_HD, RW_HD), dt),
                   jnp.zeros((DEPTH, bp, RW_COLS), dt))
    pos_p = jnp.arange(x_prompt.shape[1])
    pos_s = PAST_LEN + jnp.arange(x_sample.shape[1])
    y_prompt, (conv_p, lru_p, hgrn_p, rwkv_p, shift_p) = run_trunk(x_prompt, p_prompt, zero_states, pos_p, lb_all, W)
    y_sample, (conv_s, lru_s, hgrn_s, rwkv_s, shift_s) = run_trunk(
        x_sample, p_sample, (state_conv_a, state_lru_a, state_hgrn, state_rwkv, state_shift_c), pos_s, lb_all, W)
    return (y_prompt, y_sample, conv_p, lru_p, hgrn_p, rwkv_p, shift_p, conv_s, lru_s, hgrn_s, rwkv_s, shift_s)
```

```python
import numpy as np
import concourse.bass as bass
import concourse.mybir as mybir
from concourse.bass_utils import run_bass_kernel_spmd

F32 = mybir.dt.float32
BF16 = mybir.dt.bfloat16
AF = mybir.ActivationFunctionType
ALU = mybir.AluOpType
AX = mybir.AxisListType

ENGS = ("pe", "act", "dve", "pool", "sp")


class Tile:
    __slots__ = ("t", "name", "w", "r", "sem", "cnt", "space")

    def __init__(self, t, name, space):
        self.t = t
        self.name = name
        self.w = None
        self.r = []
        self.sem = None
        self.cnt = 0
        self.space = space

    def __getitem__(self, k):
        return self.t[k]


class Prog:
    def __init__(self, nc):
        self.nc = nc
        self.ops = []
        self.stack = None
        self.tiles = []
        self.dma_tiles = []
        self.psum_rot = 0
        self.free_semrefs = []

    def sb(self, name, shape, dt):
        t = self.stack.enter_context(self.nc.sbuf_tensor("sb_" + name, list(shape), dt))
        tl = Tile(t, name, "sb")
        self.tiles.append(tl)
        return tl

    def ps(self, name, shape, dt=F32):
        t = self.stack.enter_context(self.nc.psum_tensor("ps_" + name, list(shape), dt))
        tl = Tile(t, name, "ps")
        self.tiles.append(tl)
        return tl

    def _deps(self, eng, reads, writes):
        deps = []
        wset = set(id(t) for t in writes)
        for t in reads:
            if t.w is not None:
                deps.append(("raw", t.w))
        for t in writes:
            if t.w is not None:
                deps.append(("waw", t.w))
            for r in t.r:
                deps.append(("war", r))
        return deps

    def _commit(self, me, reads, writes):
        for t in reads:
            t.r.append(me)
        for t in writes:
            t.w = me
            t.r = []

    def op(self, eng, fn, reads=(), writes=()):
        reads = [t for t in reads if t is not None]
        writes = [t for t in writes if t is not None]
        oid = len(self.ops)
        deps = self._deps(eng, reads, writes)
        self.ops.append(dict(id=oid, eng=eng, fn=fn, deps=deps, dma=None, sig=False, sidx=None, ph=getattr(self, 'ph', '')))
        self._commit(("op", oid), reads, writes)
        return oid

    def dma(self, q, out_ap, in_ap, prim, reads=(), writes=(), **kw):
        reads = [t for t in reads if t is not None]
        writes = [t for t in writes if t is not None]
        oid = len(self.ops)
        deps = self._deps(q, reads, writes)
        if prim.sem is None:
            if self.free_semrefs:
                prim.sem = self.free_semrefs.pop()
            else:
                prim.sem = [self.stack.enter_context(self.nc.semaphore("d_%d" % len(self.dma_tiles))), 0]
                self.dma_tiles.append(prim.sem)
        prim.sem[1] += 16

        def fn(e, out_ap=out_ap, in_ap=in_ap, kw=kw):
            return e.dma_start(out=out_ap, in_=in_ap, **kw)

        self.ops.append(dict(id=oid, eng=q, fn=fn, deps=deps, dma=prim, dmasem=prim.sem, sig=False, sidx=None))
        self._commit(("dma", prim, oid), reads, writes)
        return oid

    def barrier(self):
        self.ops.append(dict(id=len(self.ops), eng=None, bar=True, deps=[], dma=None, sig=False, sidx=None))

    def emit(self):
        nc = self.nc
        ops = self.ops
        CE = ("pe", "act", "dve", "pool")
        lastop = {}
        for o in ops:
            if o.get("bar"):
                for e, i in lastop.items():
                    ops[i]["sig"] = True
                continue
            for kind, d in o["deps"]:
                if d[0] == "op":
                    p = ops[d[1]]
                    if p["eng"] == o["eng"] and o["dma"] is None:
                        if not (kind == "raw" and o["eng"] in ("act", "dve", "pool")):
                            continue
                    p["sig"] = True
            if o["dma"] is None:
                lastop[o["eng"]] = o["id"]
        cnt = {e: 0 for e in ENGS}
        for o in ops:
            if o.get("bar"):
                continue
            if o["dma"] is None and o["sig"]:
                cnt[o["eng"]] += 1
                o["sidx"] = cnt[o["eng"]]
        issued = {}
        tiles_by_id = {}
        lastsig = {e: 0 for e in CE}
        pending = {e: {} for e in ENGS}
        for o in ops:
            if o.get("bar"):
                snap = {}
                for e in CE:
                    if lastsig[e] > 0:
                        snap[("e", e)] = lastsig[e]
                for k, v in issued.items():
                    snap[("d", k)] = v
                for x in ENGS:
                    for k, v in snap.items():
                        if pending[x].get(k, 0) < v:
                            pending[x][k] = v
                continue
            w = {}
            for kind, d in o["deps"]:
                if d[0] == "op":
                    p = ops[d[1]]
                    if p["eng"] == o["eng"] and o["dma"] is None:
                        if not (kind == "raw" and o["eng"] in ("act", "dve", "pool")):
                            continue
                    k = ("e", p["eng"]); v = p["sidx"]
                else:
                    tile = d[1]
                    k = ("d", id(tile.sem)); v = issued.get(id(tile.sem), 0)
                    tiles_by_id[id(tile.sem)] = tile.sem
                if w.get(k, 0) < v:
                    w[k] = v
            pd = pending[o["eng"]]
            if pd and not o.get("nobar"):
                for k, v in pd.items():
                    if k == ("e", o["eng"]) and o["dma"] is None:
                        continue
                    if w.get(k, 0) < v:
                        w[k] = v
                pending[o["eng"]] = {}
            o["w2"] = list(w.items())
            if o["dma"] is not None:
                t = o["dmasem"]
                tiles_by_id[id(t)] = t
                issued[id(t)] = issued.get(id(t), 0) + 16
            elif o["sig"]:
                lastsig[o["eng"]] = o["sidx"]
        esem = {e: self.stack.enter_context(nc.semaphore("e_" + e)) for e in CE}
        self.final_waits = [(t[0], t[1]) for t in self.dma_tiles]
        byeng = {e: [o for o in ops if o.get("eng") == e] for e in ENGS}
        stats = {e: [len(byeng[e]), 0] for e in ENGS}

        def run(e, name):
            seen = {}
            for o in byeng[name]:
                for key, v in o["w2"]:
                    if key[0] == "e":
                        s = esem[key[1]]
                    else:
                        s = tiles_by_id[key[1]][0]
                    if seen.get(key, 0) >= v:
                        continue
                    seen[key] = v
                    e.wait_ge(s, v)
                    stats[name][1] += 1
                ins = o["fn"](e)
                if o["dma"] is not None:
                    ins.then_inc(o["dmasem"][0], 16)
                elif o["sig"]:
                    ins.then_inc(esem[name], 1)
            if name == "sp":
                for s, v in self.final_waits:
                    e.wait_ge(s, v)

        with nc.allow_non_contiguous_dma(reason="small strided state rows"), nc.Block() as block:
            @block.tensor
            def _(e):
                run(e, "pe")

            @block.scalar
            def _(e):
                run(e, "act")

            @block.vector
            def _(e):
                run(e, "dve")

            @block.gpsimd
            def _(e):
                run(e, "pool")

            @block.sync
            def _(e):
                run(e, "sp")
        self.stats = stats

NB = 256
NLAYERS = 2
PHASES = "ABCGFP"
CSTOP = 99
ALIAS_TRACK = True
DBG = False
INV = BF16
ARENA_W = 25452

_PRM = [("g_pre", 8), ("g_postmix", 8), ("g_preffn", 8), ("g_postffn", 8), ("g_ple", 8), ("conv_w", 16), ("conv_b", 4),
        ("ba", 4), ("bx", 4), ("lam", 4), ("hlb", 8), ("hg_norm_g", 4), ("mu", 14), ("w0", 4), ("a0", 4), ("k_k", 4),
        ("k_a", 4), ("r_k", 4), ("ln_g", 4), ("ln_b", 4)]
PRM_OFF = {}
_o = 0
for _n, _w in _PRM:
    PRM_OFF[_n] = _o; _o += _w
NPRM = _o
_CST = [("ident", 128), ("ones", 128), ("blk64", 128), ("blkmean", 128), ("identpair", 64), ("minc_p", 128), ("mstr_p", 128),
        ("mlow_p", 128), ("minc_s", 64), ("mstr_s", 64), ("mlow_s", 64), ("rmask_p", 256), ("rmask_s", 64), ("ind", 16), ("indp", 2)]
CST_OFF = {}
_o = 0
for _n, _w in _CST:
    CST_OFF[_n] = (_o, _o + _w); _o += _w
NCST = _o


def _make_cst():
    c = np.zeros((128, NCST), np.float32)
    def put(name, arr):
        a, b = CST_OFF[name]
        c[:arr.shape[0], a:a + arr.shape[1]] = arr
    p = np.arange(128)
    put("ident", np.eye(128, dtype=np.float32)); put("ones", np.ones((128, 128), np.float32))
    b64 = (p[:, None] // 64 == p[None, :] // 64).astype(np.float32)
    put("blk64", b64); put("blkmean", b64 / 64.0)
    put("identpair", (p[:, None] % 64 == np.arange(64)[None, :]).astype(np.float32))
    for sfx, n, C in (("p", 128, 64), ("s", 64, 4)):
        s = np.arange(n)[:, None]; t = np.arange(n)[None, :]
        same = (s // C == t // C)
        put("minc_" + sfx, (same & (s <= t)).astype(np.float32))
        put("mstr_" + sfx, (same & (s < t)).astype(np.float32))
        put("mlow_" + sfx, (same & (t < s)).astype(np.float32))
    put("rmask_p", np.tile((np.arange(256) % 64 != 0).astype(np.float32)[None, :], (128, 1)))
    put("rmask_s", np.tile((np.arange(64) % 4 != 0).astype(np.float32)[None, :], (128, 1)))
    put("ind", (np.arange(128)[:, None] // 4 == np.arange(16)[None, :]).astype(np.float32))
    put("indp", (np.arange(128)[:, None] // 64 == np.arange(2)[None, :]).astype(np.float32))
    return c

from contextlib import ExitStack

EPS = 1e-6
GN_EPS = 64e-5
DEC_C = 0.6065306597126334
NQ = 1088


class Blk:
    def __init__(self, kind, off, n, g0):
        self.kind, self.off, self.n, self.g0 = kind, off, n, g0
        if kind == "p":
            self.G, self.Tg, self.C = 1, n, 64
            self.tiles = [(t * 128, 128, [(0, 64), (64, 64)]) for t in range(n // 128)]
        else:
            self.G, self.Tg, self.C = 16, 4, 4
            self.tiles = [(0, 64, [(4 * i, 4) for i in range(16)])]
        self.nch = n // self.C


def build_program(dbg=None):
    nc = bass.Bass("TRN2", target_bir_lowering=False)
    din = lambda name, shape: nc.dram_tensor(name, list(shape), F32, kind="ExternalInput").ap()
    dout = lambda name, shape: nc.dram_tensor(name, list(shape), F32, kind="ExternalOutput").ap()
    xT_d = din("xT", [128, 8, 2112]); pT_d = din("pT", [2, 128, 2, 2112])
    conv_in = din("conv_in", [2, 128, 4, 16, 3]); lru_in = din("lru_in", [2, 128, 4, 16])
    hg_in = din("hg_in", [2, 16, 4, 128, 128]); rw_in = din("rw_in", [2, 128, 16, 4, 64])
    sh_in = din("sh_in", [2, 128, 14, 16])
    prm_d = din("prm", [2, 128, NPRM]); cst_d = din("cst", [128, NCST])
    WinT_d = din("WinT", [2, 62, 128, 8, 128]); Wbi_d = din("Wbi", [2, 128, 8, 512])
    WA_d = din("WA", [2, 4, 128, 128]); WX_d = din("WX", [2, 4, 128, 128])
    wup_d = din("wup", [2, 64, 512]); aup_d = din("aup", [2, 64, 512]); gup_d = din("gup", [2, 128, 512])
    rows_d = din("rows", [2, 2, 512])
    Wbr_d = din("Wbr", [2, 8, 3, 128, 4, 128]); Wout_d = din("Wout", [2, 128, 8, 1024])
    Wg_d = din("Wg", [2, 22, 128, 8, 128]); Wu_d = din("Wu", [2, 22, 128, 8, 128])
    Wd_d = din("Wd", [2, 8, 2, 128, 11, 128]); Wple_d = din("Wple", [2, 128, 2, 1024]); Wpg_d = din("Wpg", [2, 128, 8, 1024])
    yT_d = dout("yT", [128, 8, 2112])
    conv_o = dout("conv_o", [2, 128, 4, 17, 3]); lru_o = dout("lru_o", [2, 128, 4, 17])
    hg_o = dout("hg_o", [2, 17, 4, 128, 128]); rw_o = dout("rw_o", [2, 128, 17, 4, 64])
    sh_o = dout("sh_o", [2, 128, 14, 17])
    dbg_d = dout("dbg_br", [3, 128, 4, 64]) if DBG else None

    with ExitStack() as st:
        P = Prog(nc); P.stack = st
        tmpd = {}

        def ptmp(tag, shape, dt, bufs=2):
            key = (tag, tuple(shape), dt)
            if key not in tmpd:
                tmpd[key] = [[P.sb("%s_%d_%d" % (tag, len(tmpd), i), shape, dt) for i in range(bufs)], 0]
            e = tmpd[key]; t = e[0][e[1] % bufs]; e[1] += 1
            return t

        ARW = ARENA_W
        arena_raw = st.enter_context(nc.sbuf_tensor("arena", [128, ARW], F32))
        ar = {"off": 0, "tiles": {}, "n": 0}

        ar["map"] = []

        def arena_reset():
            if not ALIAS_TRACK:
                P.barrier()
            ar["off"] = 0
            for lst, _ in ar["tiles"].values():
                for tl in lst:
                    if tl.sem is not None:
                        P.free_semrefs.append(tl.sem)
            ar["tiles"] = {}

        def tmp(tag, shape, dt, bufs=1):
            if tag not in ar["tiles"]:
                lst = []
                nel = 1
                for s_ in shape[1:]:
                    nel *= s_
                words = (nel * (2 if dt == BF16 else 4) + 3) // 4
                for i in range(bufs):
                    assert ar["off"] + words <= ARW, ("arena overflow", tag, ar["off"], words)
                    v = arena_raw[:, ar["off"]:ar["off"] + words]; ar["off"] += words
                    if dt == BF16:
                        v = v.bitcast(BF16)[:, :nel]
                    if len(shape) == 3:
                        v = v.rearrange("p (a b) -> p a b", b=shape[2])
                    elif len(shape) == 4:
                        v = v.rearrange("p (a b c) -> p a b c", b=shape[2], c=shape[3])
                    ar["n"] += 1
                    tl = Tile(v, "%s_%d" % (tag, ar["n"]), "sb"); P.tiles.append(tl)
                    if ALIAS_TRACK:
                        a1 = ar["off"] - words; b1 = ar["off"]
                        nm = []
                        for (a0, b0, t0) in ar["map"]:
                            if a0 < b1 and b0 > a1:
                                if t0.w is not None:
                                    tl.r.append(t0.w)
                                tl.r.extend(t0.r)
                                if a0 >= a1 and b0 <= b1:
                                    continue
                            nm.append((a0, b0, t0))
                        nm.append((a1, b1, tl)); ar["map"] = nm
                    lst.append(tl)
                ar["tiles"][tag] = [lst, 0]
            e = ar["tiles"][tag]; t = e[0][e[1] % len(e[0])]; e[1] += 1
            return t

        banks = [P.ps("bank%d" % i, [128, 512], F32) for i in range(6)]
        bank_o2 = P.ps("bank_o2", [128, 512], F32)
        bankb = P.ps("bankb", [128, 1024], BF16)
        brot = [0]

        def bank():
            b = banks[brot[0] % 6]; brot[0] += 1
            return b

        def mm(ps, out, lhsT, rhs, start, stop, reads):
            P.op("pe", lambda e: e.matmul(out, lhsT=lhsT, rhs=rhs, start=start, stop=stop), reads, [ps])

        def tr(ps, out, in_, ident, reads):
            P.op("pe", lambda e: e.transpose(out, in_, ident), reads, [ps])

        def act(out, in_, func, reads, writes, scale=1.0, bias=0.0):
            P.op("act", lambda e: e.activation(out=out, in_=in_, func=func, scale=scale, bias=bias), reads, writes)

        def tt(out, a, b, op, reads, writes, eng="dve"):
            P.op(eng, lambda e: e.tensor_tensor(out=out, in0=a, in1=b, op=op), reads, writes)

        def ts(out, a, s1, s2, op0, op1, reads, writes, eng="dve"):
            if s2 is None:
                P.op(eng, lambda e: e.tensor_scalar(out=out, in0=a, scalar1=s1, scalar2=None, op0=op0), reads, writes)
            else:
                P.op(eng, lambda e: e.tensor_scalar(out=out, in0=a, scalar1=s1, scalar2=s2, op0=op0, op1=op1), reads, writes)

        def stt(out, a, s, b, op0, op1, reads, writes):
            P.op("dve", lambda e: e.scalar_tensor_tensor(out=out, in0=a, scalar=s, in1=b, op0=op0, op1=op1), reads, writes)

        def cp(out, in_, reads, writes, eng="act"):
            if eng == "act":
                P.op("act", lambda e: e.copy(out=out, in_=in_), reads, writes)
            else:
                P.op(eng, lambda e: e.tensor_copy(out=out, in_=in_), reads, writes)

        def recip(out, in_, reads, writes):
            P.op("dve", lambda e: e.reciprocal(out=out, in_=in_), reads, writes)

        def memset(ap, v, writes, eng="dve"):
            P.op(eng, lambda e: e.memset(ap, v), [], writes)

        def scan(out, d0, d1, reads, writes):
            P.op("dve", lambda e: e.tensor_tensor_scan(out=out, data0=d0, data1=d1, initial=0.0, op0=ALU.mult, op1=ALU.add), reads, writes)

        MUL, ADD, SUB, MAX = ALU.mult, ALU.add, ALU.subtract, ALU.max

        cst = P.sb("cst", [128, NCST], F32)
        P.dma("sp", cst[:], cst_d, cst, writes=[cst])
        cstb = P.sb("cstb", [128, 384], BF16)
        cp(cstb[:], cst[:, 0:384], [cst], [cstb], eng="dve")
        C_ = lambda name, bf=False: (cstb if bf else cst)[:, CST_OFF[name][0]:CST_OFF[name][1]]
        identf = C_("ident"); identb = C_("ident", True); onesb = C_("ones", True)
        blk64b = C_("blk64", True); blkmean = C_("blkmean"); identpair = C_("identpair")

        prm = [P.sb("prm%d" % L, [128, NPRM], F32) for L in range(2)]
        for L in range(2):
            P.dma("sp", prm[L][:], prm_d[L], prm[L], writes=[prm[L]])
        der = [P.sb("der%d" % L, [128, 24], F32) for L in range(2)]

        def pc(L, name, i=0):
            o = PRM_OFF[name] + i
            return prm[L][:, o:o + 1]

        for L in range(2):
            lam = prm[L][:, PRM_OFF["lam"]:PRM_OFF["lam"] + 4]
            act(der[L][:, 0:4], lam, AF.Exp, [prm[L]], [der[L]], scale=-1.0)
            act(der[L][:, 0:4], der[L][:, 0:4], AF.Ln, [der[L]], [der[L]], bias=1.0)
            ts(der[L][:, 4:8], der[L][:, 0:4], -16.0, None, MUL, None, [der[L]], [der[L]])
            ts(der[L][:, 8:12], der[L][:, 0:4], 8.0, None, MUL, None, [der[L]], [der[L]])
            ts(der[L][:, 0:4], der[L][:, 0:4], -8.0, None, MUL, None, [der[L]], [der[L]])
            if L == 0:
                memset(der[L][:, 12:16], 0.0, [der[L]])
            else:
                l0 = prm[L][:, PRM_OFF["hlb"]:PRM_OFF["hlb"] + 4]; l1 = prm[L][:, PRM_OFF["hlb"] + 4:PRM_OFF["hlb"] + 8]
                tt(der[L][:, 12:16], l1, l0, SUB, [prm[L]], [der[L]])
                act(der[L][:, 12:16], der[L][:, 12:16], AF.Sigmoid, [der[L]], [der[L]])
            ts(der[L][:, 16:20], der[L][:, 12:16], -1.0, 1.0, MUL, ADD, [der[L]], [der[L]])
            ka = prm[L][:, PRM_OFF["k_a"]:PRM_OFF["k_a"] + 4]
            ts(der[L][:, 20:24], ka, -1.0, 1.0, MUL, ADD, [prm[L]], [der[L]])
        dc = lambda L, o: der[L][:, o:o + 1]

        xT = P.sb("xT", [128, 8, NQ], F32)
        U_ = P.sb("U_", [128, 12 * NQ], BF16)
        class _V:
            def __init__(s_, i): s_.i = i
            def __getitem__(s_, k): return U_[:, s_.i * 4 * NQ:(s_.i + 1) * 4 * NQ].rearrange("p (a b) -> p a b", b=NQ)[k]
        brTv = [_V(i) for i in range(3)]
        hTp = P.sb("hTp", [128, 8, 512], BF16)
        convtail = [P.sb("ctail%d" % L, [128, 4, 3], F32) for L in range(2)]
        lruh = [P.sb("lruh%d" % L, [128, 4], F32) for L in range(2)]
        shprev = [P.sb("shprev%d" % L, [128, 14], F32) for L in range(2)]
        hgS = {}; rwM = {}
        for L in range(2):
            memset(convtail[L][:], 0.0, [convtail[L]]); memset(lruh[L][:], 0.0, [lruh[L]]); memset(shprev[L][:], 0.0, [shprev[L]])
            s = ptmp("hgS%d" % L, [128, 4, 128], F32, 2); sb_ = ptmp("hgSb%d" % L, [128, 4, 128], BF16, 3)
            memset(s[:], 0.0, [s]); memset(sb_[:], 0.0, [sb_]); hgS[L] = (s, sb_)
            m = ptmp("rwM%d" % L, [128, 4, 64], F32, 2); mb = ptmp("rwMbz%d" % L, [128, 8, 64], BF16, 3)
            memset(m[:], 0.0, [m]); memset(mb[:], 0.0, [mb]); rwM[L] = (m, mb)

        def wload(tag, src, shape, bufs=3):
            w = tmp(tag, shape, BF16, bufs)
            P.dma("pool", w[:], src, w, writes=[w])
            return w

        def rms_rstd(srcs, n, D):
            b = bank()
            for c, (ap, t) in enumerate(srcs):
                sq = tmp("sq", [128, 512], BF16, 3)
                act(sq[:, :n], ap, AF.Square, [t], [sq])
                mm(b, b[:, :n], onesb, sq[:, :n], c == 0, c == len(srcs) - 1, [cstb, sq])
            r = tmp("rstd", [128, 512], F32, 2)
            act(r[:, :n], b[:, :n], AF.Ln, [b], [r], scale=1.0 / D, bias=EPS)
            act(r[:, :n], r[:, :n], AF.Exp, [r], [r], scale=-0.5)
            return r

        def norm_to_bf16(L, gname, blk):
            n, off = blk.n, blk.off
            r = rms_rstd([(xT[:, c, off:off + n], xT) for c in range(8)], n, 1024.0)
            h = hTp
            for c in range(8):
                stt(h[:, c, :n], xT[:, c, off:off + n], pc(L, gname, c), r[:, :n], MUL, MUL, [xT, prm[L], r], [h])
            return h

        def postnorm_residual(L, gname, blk, stage, so=0):
            n, off = blk.n, blk.off
            r = rms_rstd([(stage[:, c, so:so + n], stage) for c in range(8)], n, 1024.0)
            for c in range(8):
                stt(stage[:, c, so:so + n], stage[:, c, so:so + n], pc(L, gname, c), r[:, :n], MUL, MUL, [stage, prm[L], r], [stage])
            tt(xT[:, :, off:off + n], xT[:, :, off:off + n], stage[:, :, so:so + n], ADD, [xT, stage], [xT])

        win_bufs = [P.sb("winp%d" % i, [128, 8, 128], BF16) for i in range(4)]
        win_rot = [0]

        def proj_fm(L, col, hT, n, ho=0):
            w = win_bufs[win_rot[0] % 4]; win_rot[0] += 1
            oid_ = P.dma("pool", w[:], WinT_d[L, col], w, writes=[w])
            P.ops[oid_]["nobar"] = True
            b = bank()
            for kc in range(8):
                mm(b, b[:, :n], w[:, kc, :], hT[:, kc, ho:ho + n], kc == 0, kc == 7, [w, hT])
            return b

        def phaseA(L, blk, hT, last):
            G, Tg, n, off = blk.G, blk.Tg, blk.n, blk.off
            sfx = blk.kind
            xcat = tmp("xcat" + sfx, [128, 4, G, 3 + Tg], F32, 1)
            if sfx == "p":
                cp(xcat[:, :, 0, 0:3], convtail[L][:], [convtail[L]], [xcat], eng="dve")
            else:
                for c in range(4):
                    P.dma("sp", xcat[:, c, :, 0:3], conv_in[L, :, c], xcat, writes=[xcat])
            gg = tmp("gg", [128, 4, NB], F32, 1)
            for c in range(8):
                b = proj_fm(L, c, hT, n)
                if c < 4:
                    cp(xcat[:, c, :, 3:3 + Tg], b[:, :n].rearrange("p (g t) -> p g t", t=Tg), [b], [xcat])
                else:
                    u = tmp("ga_u", [128, NB], F32, 2)
                    act(u[:, :n], b[:, :n], AF.Square, [b], [u])
                    ts(u[:, :n], u[:, :n], 0.044715, 1.0, MUL, ADD, [u], [u])
                    tt(u[:, :n], u[:, :n], b[:, :n], MUL, [u, b], [u])
                    act(u[:, :n], u[:, :n], AF.Sigmoid, [u], [u], scale=1.5957691216057308)
                    tt(gg[:, c - 4, :n], u[:, :n], b[:, :n], MUL, [u, b], [gg])
            xc = tmp("xc" + sfx, [128, 4, G, Tg], F32, 1); xcb = tmp("xcb" + sfx, [128, 4, G, Tg], BF16, 1)
            for c in range(4):
                ts(xc[:, c], xcat[:, c, :, 0:Tg], pc(L, "conv_w", c * 4), pc(L, "conv_b", c), MUL, ADD, [xcat, prm[L]], [xc])
                for j in range(1, 4):
                    stt(xc[:, c], xcat[:, c, :, j:j + Tg], pc(L, "conv_w", c * 4 + j), xc[:, c], MUL, ADD, [xcat, prm[L], xc], [xc])
            cp(xcb[:], xc[:], [xc], [xcb])
            if sfx == "p":
                cp(convtail[L][:], xcat[:, :, 0, Tg:Tg + 3], [xcat], [convtail[L]], eng="dve")
                if last:
                    P.dma("sp", conv_o[L, :, :, 0, :], convtail[L][:], convtail[L], reads=[convtail[L]])
            else:
                for c in range(4):
                    P.dma("sp", conv_o[L, :, c, 1:17, :], xcat[:, c, :, Tg:Tg + 3], xcat, reads=[xcat])
            A1 = tmp("A1" + sfx, [128, 4, G, Tg + 1], F32, 1); B1 = tmp("B1" + sfx, [128, 4, G, Tg + 1], F32, 1)
            memset(A1[:, :, :, 0:1], 0.0, [A1])
            if sfx == "p":
                cp(B1[:, :, 0, 0:1], lruh[L][:].unsqueeze(2), [lruh[L]], [B1], eng="dve")
            else:
                lrs = tmp("lrs", [128, 4, 16], F32, 1)
                P.dma("sp", lrs[:], lru_in[L], lrs, writes=[lrs])
                cp(B1[:, :, :, 0], lrs[:], [lrs], [B1], eng="dve")
            wa = wload("wa", WA_d[L].rearrange("c p m -> p c m"), [128, 4, 128], 2)
            wx = wload("wx", WX_d[L].rearrange("c p m -> p c m"), [128, 4, 128], 2)
            v3 = lambda t_: t_[:, :n].rearrange("p (g t) -> p g t", t=Tg)
            for c in range(4):
                xcbf = xcb[:, c].rearrange("p g t -> p (g t)")
                b1 = bank(); mm(b1, b1[:, :n], wa[:, c, :], xcbf, True, True, [wa, xcb])
                rt = tmp("rt", [128, NB], F32, 2)
                act(rt[:, :n], b1[:, :n], AF.Sigmoid, [b1, prm[L]], [rt], bias=pc(L, "ba", c))
                b2 = bank(); mm(b2, b2[:, :n], wx[:, c, :], xcbf, True, True, [wx, xcb])
                it = tmp("it", [128, NB], F32, 2)
                act(it[:, :n], b2[:, :n], AF.Sigmoid, [b2, prm[L]], [it], bias=pc(L, "bx", c))
                act(A1[:, c, :, 1:], v3(rt), AF.Exp, [rt, der[L]], [A1], scale=dc(L, 0 + c))
                t1 = tmp("lt1", [128, NB], F32, 2); t2 = tmp("lt2", [128, NB], F32, 2)
                act(t1[:, :n], rt[:, :n], AF.Exp, [rt, der[L]], [t1], scale=dc(L, 4 + c))
                act(t2[:, :n], rt[:, :n], AF.Tanh, [rt, der[L]], [t2], scale=dc(L, 8 + c))
                stt(t1[:, :n], t1[:, :n], 1.0, t2[:, :n], ADD, MUL, [t1, t2], [t1])
                act(t1[:, :n], t1[:, :n], AF.Sqrt, [t1], [t1])
                if sfx == "p" and blk.g0 == 0:
                    memset(t1[:, 0:1], 1.0, [t1])
                tt(t2[:, :n], xc[:, c].rearrange("p g t -> p (g t)"), it[:, :n], MUL, [xc, it], [t2])
                tt(B1[:, c, :, 1:], v3(t2), v3(t1), MUL, [t1, t2], [B1])
            H1 = tmp("H1" + sfx, [128, 4, G, Tg + 1], F32, 1)
            for c in range(4):
                fl = lambda t_: t_[:, c].rearrange("p g t -> p (g t)")
                scan(fl(H1), fl(A1), fl(B1), [A1, B1], [H1])
            for c in range(4):
                tt(brTv[0][:, c, off:off + n].rearrange("p (g t) -> p g t", t=Tg), H1[:, c, :, 1:],
                   gg[:, c, :n].rearrange("p (g t) -> p g t", t=Tg), MUL, [H1, gg], [U_])
            if sfx == "p":
                cp(lruh[L][:].unsqueeze(2), H1[:, :, 0, Tg:Tg + 1], [H1], [lruh[L]], eng="dve")
                if last:
                    P.dma("sp", lru_o[L, :, :, 0], lruh[L][:], lruh[L], reads=[lruh[L]])
            else:
                lro = tmp("lro", [128, 4, 16], F32, 1)
                cp(lro[:], H1[:, :, :, Tg], [H1], [lro], eng="dve")
                P.dma("sp", lru_o[L, :, :, 1:17], lro[:], lro, reads=[lro])

        def phaseB(L, blk, hT, last):
            G, Tg, n, off, C, nch = blk.G, blk.Tg, blk.n, blk.off, blk.C, blk.nch
            sfx = blk.kind
            rmask = C_("rmask_" + sfx)[:, :n]
            qs = tmp("hq", [128, 4, NB], F32, 1); ff = tmp("hf", [128, 4, NB], F32, 1)
            kk = tmp("hk", [128, 4, NB], F32, 1); gs = tmp("hg", [128, 4, NB], F32, 1)
            bc = tmp("hbc", [128, 4, NB], F32, 1); ebc = tmp("hebc", [128, 4, NB], F32, 1)
            for h in range(4):
                b = proj_fm(L, 8 + h, hT, n)
                act(qs[:, h, :n], b[:, :n], AF.Silu, [b], [qs])
            for h in range(4):
                b = proj_fm(L, 12 + h, hT, n)
                act(ff[:, h, :n], b[:, :n], AF.Sigmoid, [b], [ff])
                ts(ff[:, h, :n], ff[:, h, :n], dc(L, 16 + h), dc(L, 12 + h), MUL, ADD, [ff, der[L]], [ff])
                ts(ff[:, h, :n], ff[:, h, :n], 1e-20, None, MAX, None, [ff], [ff])
                ts(kk[:, h, :n], ff[:, h, :n], -1.0, 1.0, MUL, ADD, [ff], [kk])
                act(ff[:, h, :n], ff[:, h, :n], AF.Ln, [ff], [ff])
                scan(bc[:, h, :n], rmask, ff[:, h, :n], [cst, ff], [bc])
            for h in range(4):
                b = proj_fm(L, 20 + h, hT, n)
                act(gs[:, h, :n], b[:, :n], AF.Silu, [b], [gs])
            qt = tmp("hqt", [128, 4, NB], BF16, 1); kt = tmp("hkt", [128, 4, NB], BF16, 1); kd = tmp("hkd", [128, 4, NB], BF16, 1)
            act(ebc[:, :, :n], bc[:, :, :n], AF.Exp, [bc], [ebc])
            tt(qt[:, :, :n], qs[:, :, :n], ebc[:, :, :n], MUL, [qs, ebc], [qt])
            e2 = tmp("he2", [128, 4, NB], F32, 1)
            act(e2[:, :, :n], bc[:, :, :n], AF.Exp, [bc], [e2], scale=-1.0)
            tt(kt[:, :, :n], kk[:, :, :n], e2[:, :, :n], MUL, [kk, e2], [kt])
            for h in range(4):
                bv = bc[:, h, :n].rearrange("p (c t) -> p c t", t=C)
                tt(e2[:, h, :n].rearrange("p (c t) -> p c t", t=C), bv[:, :, C - 1:C].to_broadcast([128, nch, C]), bv, SUB, [bc], [e2])
            act(e2[:, :, :n], e2[:, :, :n], AF.Exp, [e2], [e2])
            tt(kd[:, :, :n], kk[:, :, :n], e2[:, :, :n], MUL, [kk, e2], [kd])
            wbi = wload("wbi", Wbi_d[L], [128, 8, 512], 1)
            for (toff, nt, chunks) in blk.tiles:
                tsl = slice(toff, toff + nt)
                b = bank()
                for kc in range(8):
                    mm(b, b[:nt, :], hT[:, kc, tsl], wbi[:, kc, :], kc == 0, kc == 7, [hT, wbi])
                vt = tmp("hvt", [128, 512], BF16, 2)
                cp(vt[:nt, :], b[:nt, :], [b], [vt])
                for h in range(4):
                    tr(bankb, bankb[:nt, h * 128:(h + 1) * 128], kd[:, h, tsl], identb, [kd, cstb])
                kdt = tmp("hkdt", [128, 512], BF16, 2)
                cp(kdt[:nt, :], bankb[:nt, 0:512], [bankb], [kdt], eng="dve")
                Sb_list = []
                if sfx == "p":
                    S, Sb = hgS[L]
                    for (r0, Cc) in chunks:
                        Sb_list.append(Sb)
                        bs = bank()
                        for h in range(4):
                            mm(bs, bs[:, h * 128:(h + 1) * 128], kdt[r0:r0 + Cc, h * 128:(h + 1) * 128], vt[r0:r0 + Cc, h * 128:(h + 1) * 128], True, True, [kdt, vt])
                        S2 = ptmp("hgS%d" % L, [128, 4, 128], F32, 2); Sb2 = ptmp("hgSb%d" % L, [128, 4, 128], BF16, 3)
                        col = toff + r0 + Cc - 1
                        for h in range(4):
                            stt(S2[:, h, :], S[:, h, :], ebc[:, h, col:col + 1], bs[:, h * 128:(h + 1) * 128], MUL, ADD, [S, ebc, bs], [S2])
                        cp(Sb2[:], S2[:], [S2], [Sb2])
                        S, Sb = S2, Sb2
                    hgS[L] = (S, Sb)
                    if last and toff + nt == n:
                        P.dma("sp", hg_o[L, 0].rearrange("h k v -> k h v"), S[:], S, reads=[S])
                else:
                    for i, (r0, Cc) in enumerate(chunks):
                        if i == 0:
                            nxtS = tmp("hSin", [128, 4, 128], F32, 3)
                            P.dma("sp", nxtS[:], hg_in[L, 0].rearrange("h k v -> k h v"), nxtS, writes=[nxtS])
                        Sin = nxtS
                        if i + 1 < len(chunks):
                            nxtS = tmp("hSin", [128, 4, 128], F32, 3)
                            P.dma("sp", nxtS[:], hg_in[L, i + 1].rearrange("h k v -> k h v"), nxtS, writes=[nxtS])
                        Sbi = tmp("hSbi", [128, 4, 128], BF16, 16)
                        cp(Sbi[:], Sin[:], [Sin], [Sbi])
                        Sb_list.append(Sbi)
                        kdm = tmp("hkdm", [128, 512], BF16, 3)
                        ts(kdm[:nt, :], kdt[:nt, :], C_("ind")[:nt, i:i + 1], None, MUL, None, [kdt, cst], [kdm])
                        bs = bank()
                        for h in range(4):
                            mm(bs, bs[:, h * 128:(h + 1) * 128], kdm[:nt, h * 128:(h + 1) * 128], vt[:nt, h * 128:(h + 1) * 128], True, True, [kdm, vt])
                        So = tmp("hSo", [128, 4, 128], F32, 3)
                        col = toff + r0 + Cc - 1
                        for h in range(4):
                            stt(So[:, h, :], Sin[:, h, :], ebc[:, h, col:col + 1], bs[:, h * 128:(h + 1) * 128], MUL, ADD, [Sin, ebc, bs], [So])
                        P.dma("sp", hg_o[L, 1 + i].rearrange("h k v -> k h v"), So[:], So, reads=[So])
                ba = bank()
                for h in range(4):
                    mm(ba, ba[:nt, h * 128:h * 128 + nt], kt[:, h, tsl], qt[:, h, tsl], True, True, [kt, qt])
                At = tmp("hAt", [128, 4, 128], BF16, 2)
                mk = C_("minc_" + sfx)[:nt, :nt]
                tt(At[:nt, :, :nt], ba[:nt, :].rearrange("p (h t) -> p h t", t=128)[:, :, :nt], mk.unsqueeze(1).to_broadcast([nt, 4, nt]), MUL, [ba, cst], [At])
                bo = bank()
                for h in range(4):
                    mm(bo, bo[:, h * 128:h * 128 + nt], vt[:nt, h * 128:(h + 1) * 128], At[:nt, h, :nt], True, False, [vt, At])
                    for i, (r0, Cc) in enumerate(chunks):
                        mm(bo, bo[:, h * 128 + r0:h * 128 + r0 + Cc], Sb_list[i][:, h, :], qt[:, h, toff + r0:toff + r0 + Cc], False, i == len(chunks) - 1, [Sb_list[i], qt])
                bo3 = bo[:, :].rearrange("p (h t) -> p h t", t=128)[:, :, :nt]
                sq = tmp("hsq", [128, 4, 128], BF16, 1)
                act(sq[:, :, :nt], bo3, AF.Square, [bo], [sq])
                bn = bank()
                for h in range(4):
                    mm(bn, bn[:, h * 128:h * 128 + nt], onesb, sq[:, h, :nt], True, True, [cstb, sq])
                bn3 = bn[:, :].rearrange("p (h t) -> p h t", t=128)[:, :, :nt]
                rs = tmp("hrs", [128, 4, 128], F32, 1)
                act(rs[:, :, :nt], bn3, AF.Ln, [bn], [rs], scale=1.0 / 128, bias=EPS)
                act(rs[:, :, :nt], rs[:, :, :nt], AF.Exp, [rs], [rs], scale=-0.5)
                tt(rs[:, :, :nt], bo3, rs[:, :, :nt], MUL, [bo, rs], [rs])
                for h in range(4):
                    stt(brTv[1][:, h, off + toff:off + toff + nt], rs[:, h, :nt], pc(L, "hg_norm_g", h), gs[:, h, tsl], MUL, MUL, [rs, prm[L], gs], [U_])

        def phaseC_proj(L, blk, hT, last):
            G, Tg, n = blk.G, blk.Tg, blk.n
            sfx = blk.kind
            cT = tmp("cT" + sfx, [128, 14, G, Tg + 1], F32, 1)
            if sfx == "p":
                cp(cT[:, :, 0, 0:1], shprev[L][:].unsqueeze(2), [shprev[L]], [cT], eng="dve")
            else:
                shs = tmp("shs", [128, 14, 16], F32, 1)
                P.dma("sp", shs[:], sh_in[L], shs, writes=[shs])
                cp(cT[:, :, :, 0], shs[:], [shs], [cT], eng="dve")
            for c in range(14):
                b = proj_fm(L, 24 + c, hT, n, 0)
                cp(cT[:, c, :, 1:], b[:, :n].rearrange("p (g t) -> p g t", t=Tg), [b], [cT])
            if sfx == "p":
                cp(shprev[L][:].unsqueeze(2), cT[:, :, 0, Tg:Tg + 1], [cT], [shprev[L]], eng="dve")
                if last:
                    P.dma("sp", sh_o[L, :, :, 0], shprev[L][:], shprev[L], reads=[shprev[L]])
            else:
                sho = tmp("sho", [128, 14, 16], F32, 1)
                cp(sho[:], cT[:, :, :, Tg], [cT], [sho], eng="dve")
                P.dma("sp", sh_o[L, :, :, 1:17], sho[:], sho, reads=[sho])
            return cT

        def phaseC(L, blk, cT, c0, last):
            G, Tg, n, off, C, nch = blk.G, blk.Tg, blk.n, blk.off, blk.C, blk.nch
            sfx = blk.kind
            rmask = C_("rmask_" + sfx)[:, :n]
            xm = tmp("xm", [128, 14, 128], F32, 1)
            xm4 = lambda c: xm[:, c, :n].rearrange("p (g t) -> p g t", t=Tg)
            tt(xm[:, :, :n].rearrange("p c (g t) -> p c g t", t=Tg), cT[:, :, :, c0:c0 + Tg], cT[:, :, :, c0 + 1:c0 + 1 + Tg], SUB, [cT], [xm])
            for c in range(14):
                stt(xm4(c), xm4(c), pc(L, "mu", c), cT[:, c, :, c0 + 1:c0 + 1 + Tg], MUL, ADD, [xm, prm[L], cT], [xm])
            lw = tmp("lw", [128, 128], BF16, 1); sgl = tmp("sgl", [128, 128], BF16, 1)
            act(lw[0:64, :n], xm[0:64, 12, :n], AF.Tanh, [xm], [lw])
            cp(lw[64:128, :n], xm[64:128, 12, :n], [xm], [lw])
            act(sgl[:, :n], xm[:, 13, :n], AF.Sigmoid, [xm], [sgl])
            lo = tmp("lora", [128, 3, 512], BF16, 1)
            P.dma("pool", lo[0:64, 0, :], wup_d[L], lo, writes=[lo])
            P.dma("pool", lo[64:128, 1, :], aup_d[L], lo, writes=[lo])
            P.dma("pool", lo[:, 2, :], gup_d[L], lo, writes=[lo])
            sw = tmp("sw", [128, 4, 128], F32, 1); aT = tmp("aT", [128, 4, 128], F32, 1); gT = tmp("gT", [128, 4, 128], F32, 1)
            for c in range(4):
                cs = slice(c * 128, (c + 1) * 128)
                b = bank(); mm(b, b[:, :n], lo[0:64, 0, cs], lw[0:64, :n], True, True, [lo, lw])
                act(sw[:, c, :n], b[:, :n], AF.Sigmoid, [b, prm[L]], [sw], bias=pc(L, "w0", c))
                b = bank(); mm(b, b[:, :n], lo[64:128, 1, cs], lw[64:128, :n], True, True, [lo, lw])
                act(aT[:, c, :n], b[:, :n], AF.Sigmoid, [b, prm[L]], [aT], bias=pc(L, "a0", c))
                b = bank(); mm(b, b[:, :n], lo[:, 2, cs], sgl[:, :n], True, True, [lo, sgl])
                cp(gT[:, c, :n], b[:, :n], [b], [gT])
            rT = lambda c: xm[:, c, :n]; kT = lambda c: xm[:, 4 + c, :n]; vT = lambda c: xm[:, 8 + c, :n]
            kkn = tmp("kkn", [128, 4, 128], F32, 1); kp = tmp("kp", [128, 4, 128], F32, 1); bet = tmp("bet", [128, 4, 128], F32, 1)
            t4 = tmp("t4", [128, 4, 128], F32, 1); sqb = tmp("sqb", [128, 4, 128], BF16, 1)
            for c in range(4):
                act(kkn[:, c, :n], kT(c), AF.Identity, [xm, prm[L]], [kkn], scale=pc(L, "k_k", c))
            tt(sqb[:, :, :n], kkn[:, :, :n], kkn[:, :, :n], MUL, [kkn], [sqb])
            for c in range(4):
                b = bank(); mm(b, b[:, :n], blk64b, sqb[:, c, :n], True, True, [cstb, sqb])
                act(t4[:, c, :n], b[:, :n], AF.Sqrt, [b], [t4])
            ts(t4[:, :, :n], t4[:, :, :n], 1e-12, None, MAX, None, [t4], [t4])
            act(t4[:, :, :n], t4[:, :, :n], AF.Ln, [t4], [t4])
            act(t4[:, :, :n], t4[:, :, :n], AF.Exp, [t4], [t4], scale=-1.0)
            tt(kkn[:, :, :n], kkn[:, :, :n], t4[:, :, :n], MUL, [kkn, t4], [kkn])
            for c in range(4):
                act(t4[:, c, :n], aT[:, c, :n], AF.Identity, [aT, prm[L], der[L]], [t4], scale=pc(L, "k_a", c), bias=dc(L, 20 + c))
            tt(kp[:, :, :n], xm[:, 4:8, :n], t4[:, :, :n], MUL, [xm, t4], [kp])
            tt(bet[:, :, :n], kkn[:, :, :n], aT[:, :, :n], MUL, [kkn, aT], [bet])
            bon = tmp("bon", [128, 4, 128], F32, 1)
            for c in range(4):
                act(t4[:, c, :n], rT(c), AF.Identity, [xm, prm[L]], [t4], scale=pc(L, "r_k", c))
            tt(sqb[:, :, :n], t4[:, :, :n], kp[:, :, :n], MUL, [t4, kp], [sqb])
            for c in range(4):
                b = bank(); mm(b, b[:, :n], blk64b, sqb[:, c, :n], True, True, [cstb, sqb])
                tt(bon[:, c, :n], b[:, :n], vT(c), MUL, [b, xm], [bon])
            lp = tmp("lp", [128, 4, 128], F32, 1)
            for c in range(4):
                scan(lp[:, c, :n], rmask, sw[:, c, :n], [cst, sw], [lp])
            einc = tmp("einc", [128, 4, 128], F32, 1)
            act(einc[:, :, :n], lp[:, :, :n], AF.Exp, [lp], [einc], scale=-DEC_C)
            atl = tmp("atl", [128, 4, 128], BF16, 1); btl = tmp("btl", [128, 4, 128], BF16, 1)
            ktl = tmp("ktl", [128, 4, 128], BF16, 1); rtl = tmp("rtl", [128, 4, 128], BF16, 1)
            bdc = tmp("bdc", [128, 4, 128], BF16, 1); kdc = tmp("kdc", [128, 4, 128], BF16, 1); vbf = tmp("vbf", [128, 4, 128], BF16, 1)
            tt(rtl[:, :, :n], xm[:, 0:4, :n], einc[:, :, :n], MUL, [xm, einc], [rtl])
            cp(vbf[:, :, :n], xm[:, 8:12, :n], [xm], [vbf])
            act(t4[:, :, :n], lp[:, :, :n], AF.Exp, [lp], [t4], scale=DEC_C)
            tt(btl[:, :, :n], bet[:, :, :n], t4[:, :, :n], MUL, [bet, t4], [btl])
            tt(ktl[:, :, :n], kp[:, :, :n], t4[:, :, :n], MUL, [kp, t4], [ktl])
            tt(t4[:, :, :n], lp[:, :, :n], sw[:, :, :n], SUB, [lp, sw], [t4])
            act(t4[:, :, :n], t4[:, :, :n], AF.Exp, [t4], [t4], scale=-DEC_C)
            stt(atl[:, :, :n], kkn[:, :, :n], -1.0, t4[:, :, :n], MUL, MUL, [kkn, t4], [atl])
            lv = lp[:, :, :n].rearrange("p c (k t) -> p c k t", t=C)
            tt(t4[:, :, :n].rearrange("p c (k t) -> p c k t", t=C), lv[:, :, :, C - 1:C].to_broadcast([128, 4, nch, C]), lv, SUB, [lp], [t4])
            act(t4[:, :, :n], t4[:, :, :n], AF.Exp, [t4], [t4], scale=-DEC_C)
            tt(bdc[:, :, :n], bet[:, :, :n], t4[:, :, :n], MUL, [bet, t4], [bdc])
            tt(kdc[:, :, :n], kp[:, :, :n], t4[:, :, :n], MUL, [kp, t4], [kdc])
            if CSTOP <= 1:
                return
            hp = lambda h: slice((h % 2) * 64, (h % 2) * 64 + 64)
            for (toff, nt, chunks) in blk.tiles:
                tsl = slice(toff, toff + nt)
                tok = {}
                for name, src in (("v", vbf), ("a", atl), ("bd", bdc), ("kd", kdc)):
                    for c in range(4):
                        tr(bankb, bankb[:nt, c * 128:(c + 1) * 128], src[:, c, tsl], identb, [src, cstb])
                    d = tmp("tok_" + name, [128, 512], BF16, 1)
                    cp(d[:nt, :], bankb[:nt, 0:512], [bankb], [d], eng="dve")
                    tok[name] = d
                vtk, atk, bdt, kdt = tok["v"], tok["a"], tok["bd"], tok["kd"]
                pre_m = []
                if sfx == "p":
                    for i_, (r0_, Cc_) in enumerate(chunks):
                        ind_ = C_("indp")[:nt, i_:i_ + 1]
                        bdm_ = tmp("bdm", [128, 512], BF16, 2); kdm_ = tmp("kdm", [128, 512], BF16, 2)
                        ts(bdm_[:nt, :], bdt[:nt, :], ind_, None, MUL, None, [bdt, cst], [bdm_])
                        ts(kdm_[:nt, :], kdt[:nt, :], ind_, None, MUL, None, [kdt, cst], [kdm_])
                        pre_m.append((bdm_, kdm_))
                if CSTOP <= 1.3:
                    continue
                Z = tmp("Z", [128, 8, 128], INV, 1)
                cp(Z[:nt, :, 64:128], atk[:nt, :].rearrange("p (h v) -> p h v", v=64), [atk], [Z], eng="dve")
                A4 = tmp("A4", [128, 8, 4, 128], BF16, 1)
                Nt = tmp("Nt", [128, 8, 128], INV, 1)
                mstr = C_("mstr_" + sfx)[:nt, :nt]; minc = C_("minc_" + sfx)[:nt, :nt]; mlow = C_("mlow_" + sfx)[:nt, :nt]
                for h in range(8):
                    c = h // 2; pp = hp(h)
                    b = bank()
                    mm(b, b[:nt, 0:nt], btl[pp, c, tsl], atl[pp, c, tsl], True, True, [btl, atl])
                    mm(b, b[:nt, 128:128 + nt], btl[pp, c, tsl], rtl[pp, c, tsl], True, True, [btl, rtl])
                    mm(b, b[:nt, 256:256 + nt], ktl[pp, c, tsl], atl[pp, c, tsl], True, True, [ktl, atl])
                    mm(b, b[:nt, 384:384 + nt], ktl[pp, c, tsl], rtl[pp, c, tsl], True, True, [ktl, rtl])
                    b4 = b[:nt, :].rearrange("p (k t) -> p k t", t=128)
                    tt(A4[:nt, h, 0::2, :nt], b4[:, 0::2, :nt], mstr.unsqueeze(1).to_broadcast([nt, 2, nt]), MUL, [b, cst], [A4])
                    tt(A4[:nt, h, 1::2, :nt], b4[:, 1::2, :nt], minc.unsqueeze(1).to_broadcast([nt, 2, nt]), MUL, [b, cst], [A4])
                for q in range(2):
                    b = bank()
                    for h in range(q, 8, 2):
                        c = h // 2; pp = hp(h)
                        mm(b, b[:nt, c * 128:c * 128 + nt], atl[pp, c, tsl], btl[pp, c, tsl], True, True, [atl, btl])
                    tt(Nt[:nt, q::2, :nt], b[:nt, :].rearrange("p (h t) -> p h t", t=128)[:, :, :nt],
                       mlow.unsqueeze(1).to_broadcast([nt, 4, nt]), MUL, [b, cst], [Nt])
                bx_ = bank()
                for h in range(8):
                    mm(bx_, bx_[:nt, h * 64:(h + 1) * 64], A4[:nt, h, 2, :nt], vtk[:nt, h * 64:(h + 1) * 64], True, True, [A4, vtk])
                cp(Z[:nt, :, 0:64], bx_[:nt, :].rearrange("p (h v) -> p h v", v=64), [bx_], [Z])
                if CSTOP <= 2:
                    continue
                S_ = tmp("Sinv", [128, 8, 128], INV, 1)
                tt(S_[:nt, :, :nt], A4[:nt, :, 0, :nt], identf[:nt, :nt].unsqueeze(1).to_broadcast([nt, 8, nt]), ADD, [A4, cst], [S_])
                Pm, Qm = None, Nt
                Pat = lambda h_: (A4[:nt, h_, 0, :nt] if Pm is None else Pm[:nt, h_, :nt])
                Ptile = lambda: (A4 if Pm is None else Pm)
                nlev = 5 if sfx == "p" else 1
                for lev in range(nlev):
                    Qn = tmp("Qn", [128, 8, 128], INV, 2)
                    Pn = tmp("Pn", [128, 8, 128], INV, 2) if lev < nlev - 1 else None
                    for hh in range(2):
                        b = bank()
                        for h in range(4 * hh, 4 * hh + 4):
                            mm(b, b[:nt, (h % 4) * 128:(h % 4) * 128 + nt], Pat(h), Qm[:nt, h, :nt], True, True, [Ptile(), Qm])
                        cp(Qn[:nt, 4 * hh:4 * hh + 4, :nt], b[:nt, :].rearrange("p (h t) -> p h t", t=128)[:, :, :nt], [b], [Qn])
                        if Pn is not None:
                            b = bank()
                            for h in range(4 * hh, 4 * hh + 4):
                                mm(b, b[:nt, (h % 4) * 128:(h % 4) * 128 + nt], Qm[:nt, h, :nt], Pat(h), True, True, [Ptile(), Qm])
                            cp(Pn[:nt, 4 * hh:4 * hh + 4, :nt], b[:nt, :].rearrange("p (h t) -> p h t", t=128)[:, :, :nt], [b], [Pn], eng="dve")
                    for hh in range(2):
                        b = bank()
                        for h in range(4 * hh, 4 * hh + 4):
                            mm(b, b[:nt, (h % 4) * 128:(h % 4) * 128 + nt], Qn[:nt, h, :nt], S_[:nt, h, :nt], True, True, [Qn, S_])
                        tt(S_[:nt, 4 * hh:4 * hh + 4, :nt], S_[:nt, 4 * hh:4 * hh + 4, :nt], b[:nt, :].rearrange("p (h t) -> p h t", t=128)[:, :, :nt], ADD, [S_, b], [S_])
                    Pm, Qm = Pn, Qn
                if CSTOP <= 3:
                    continue
                UW = tmp("UW", [128, 8, 128], BF16, 1)
                for hh in range(2):
                    b = bank()
                    for h in range(4 * hh, 4 * hh + 4):
                        mm(b, b[:nt, (h % 4) * 128:(h % 4) * 128 + 128], S_[:nt, h, :nt], Z[:nt, h, :], True, True, [S_, Z])
                    cp(UW[:nt, 4 * hh:4 * hh + 4, :], b[:nt, :].rearrange("p (h t) -> p h t", t=128), [b], [UW])
                b = bank()
                for h in range(8):
                    mm(b, b[hp(h), (h // 2) * 128:(h // 2) * 128 + nt], UW[:nt, h, 64:128], A4[:nt, h, 1, :nt], True, True, [UW, A4])
                Qe = tmp("Qe", [128, 4, 128], BF16, 1)
                tt(Qe[:, :, :nt], b[:, :].rearrange("p (c t) -> p c t", t=128)[:, :, :nt], rtl[:, :, tsl], ADD, [b, rtl], [Qe])
                if CSTOP <= 4:
                    continue
                bo2 = bank_o2
                if sfx == "p":
                    M, Mbz = rwM[L]
                for i, (r0, Cc) in enumerate(chunks):
                    col = toff + r0 + Cc - 1
                    ind = (C_("indp") if sfx == "p" else C_("ind"))[:nt, i:i + 1]
                    uwm = tmp("uwm", [128, 8, 128], BF16, 2)
                    ts(uwm[:nt], UW[:nt], ind, None, MUL, None, [UW, cst], [uwm])
                    if pre_m:
                        bdm, kdm = pre_m[i]
                    else:
                        bdm = tmp("bdm", [128, 512], BF16, 2); kdm = tmp("kdm", [128, 512], BF16, 2)
                        ts(bdm[:nt, :], bdt[:nt, :], ind, None, MUL, None, [bdt, cst], [bdm])
                        ts(kdm[:nt, :], kdt[:nt, :], ind, None, MUL, None, [kdt, cst], [kdm])
                    if sfx == "p":
                        Mcur, Mbzc = M, Mbz
                    else:
                        if i == 0:
                            nxtM = tmp("rMin", [128, 4, 64], F32, 3)
                            P.dma("sp", nxtM[:], rw_in[L, :, 0], nxtM, writes=[nxtM])
                        Mcur = nxtM
                        if i + 1 < len(chunks):
                            nxtM = tmp("rMin", [128, 4, 64], F32, 3)
                            P.dma("sp", nxtM[:], rw_in[L, :, i + 1], nxtM, writes=[nxtM])
                        Mbzc = tmp("rMinb", [128, 8, 64], BF16, 2)
                        memset(Mbzc[:], 0.0, [Mbzc])
                        for q in range(2):
                            cp(Mbzc[q * 64:(q + 1) * 64, q::2, :], Mcur[q * 64:(q + 1) * 64, :, :], [Mcur], [Mbzc])
                    for h in range(8):
                        c = h // 2
                        mm(bo2, bo2[hp(h), c * 128 + r0:c * 128 + r0 + Cc], Mbzc[:, h, :], Qe[:, c, r0:r0 + Cc], True, True, [Mbzc, Qe])
                    bg = bank()
                    for h in range(8):
                        mm(bg, bg[hp(h), (h // 2) * 64:(h // 2) * 64 + 64], uwm[:nt, h, 64:128], bdt[:nt, h * 64:(h + 1) * 64], True, True, [uwm, bdt])
                    GT = tmp("GT", [128, 8, 64], F32, 2)
                    memset(GT[:], 0.0, [GT])
                    for h in range(8):
                        stt(GT[hp(h), h, :], identpair[hp(h), :], einc[hp(h), h // 2, col:col + 1], bg[hp(h), (h // 2) * 64:(h // 2) * 64 + 64], MUL, ADD, [cst, einc, bg], [GT])
                    bm = bank()
                    for h in range(8):
                        o_ = bm[hp(h), (h // 2) * 64:(h // 2) * 64 + 64]
                        mm(bm, o_, bdm[:nt, h * 64:(h + 1) * 64], UW[:nt, h, 0:64], True, False, [bdm, UW])
                        mm(bm, o_, kdm[:nt, h * 64:(h + 1) * 64], vtk[:nt, h * 64:(h + 1) * 64], False, False, [kdm, vtk])
                        mm(bm, o_, GT[:, h, :], Mcur[:, h // 2, :], False, True, [GT, Mcur])
                    bm3 = bm[:, 0:256].rearrange("p (c v) -> p c v", v=64)
                    if sfx == "p":
                        M2 = ptmp("rwM%d" % L, [128, 4, 64], F32, 2); Mbz2 = ptmp("rwMbz%d" % L, [128, 8, 64], BF16, 3)
                        cp(M2[:], bm3, [bm], [M2], eng="dve")
                        memset(Mbz2[:], 0.0, [Mbz2])
                        for q in range(2):
                            cp(Mbz2[q * 64:(q + 1) * 64, q::2, :], bm[q * 64:(q + 1) * 64, 0:256].rearrange("p (c v) -> p c v", v=64), [bm], [Mbz2])
                        M, Mbz = M2, Mbz2
                    else:
                        Mout = tmp("rMout", [128, 4, 64], F32, 2)
                        cp(Mout[:], bm3, [bm], [Mout], eng="dve")
                        P.dma("sp", rw_o[L, :, 1 + i], Mout[:], Mout, reads=[Mout])
                if sfx == "p":
                    rwM[L] = (M, Mbz)
                    if last and toff + nt == n:
                        P.dma("sp", rw_o[L, :, 0], M[:], M, reads=[M])
                if CSTOP <= 5:
                    continue
                bo = bank()
                for h in range(8):
                    c = h // 2
                    o_ = bo[hp(h), c * 128:c * 128 + nt]
                    mm(bo, o_, UW[:nt, h, 0:64], A4[:nt, h, 1, :nt], True, False, [UW, A4])
                    mm(bo, o_, vtk[:nt, h * 64:(h + 1) * 64], A4[:nt, h, 3, :nt], False, True, [vtk, A4])
                bo3 = bo[:, :].rearrange("p (c t) -> p c t", t=128)[:, :, :nt]
                oS = tmp("oS", [128, 4, 128], F32, 1); dv = tmp("dv", [128, 4, 128], F32, 1)
                cp(oS[:, :, :nt], bo3, [bo], [oS])
                tt(oS[:, :, :nt], oS[:, :, :nt], bo2[:, :].rearrange("p (c t) -> p c t", t=128)[:, :, :nt], ADD, [oS, bo2], [oS])
                b1 = bank()
                for c in range(4):
                    mm(b1, b1[:, c * 128:c * 128 + nt], blkmean, oS[:, c, :nt], True, True, [cst, oS])
                tt(dv[:, :, :nt], oS[:, :, :nt], b1[:, :].rearrange("p (c t) -> p c t", t=128)[:, :, :nt], SUB, [oS, b1], [dv])
                tt(oS[:, :, :nt], dv[:, :, :nt], dv[:, :, :nt], MUL, [dv], [oS])
                b2 = bank()
                for c in range(4):
                    mm(b2, b2[:, c * 128:c * 128 + nt], blkmean, oS[:, c, :nt], True, True, [cst, oS])
                act(oS[:, :, :nt], b2[:, :].rearrange("p (c t) -> p c t", t=128)[:, :, :nt], AF.Ln, [b2], [oS], bias=GN_EPS)
                act(oS[:, :, :nt], oS[:, :, :nt], AF.Exp, [oS], [oS], scale=-0.5)
                tt(dv[:, :, :nt], dv[:, :, :nt], oS[:, :, :nt], MUL, [dv, oS], [dv])
                for c in range(4):
                    act(dv[:, c, :nt], dv[:, c, :nt], AF.Identity, [dv, prm[L]], [dv], scale=pc(L, "ln_g", c), bias=pc(L, "ln_b", c))
                tt(dv[:, :, :nt], dv[:, :, :nt], bon[:, :, tsl], ADD, [dv, bon], [dv])
                tt(brTv[2][:, :, off + toff:off + toff + nt], dv[:, :, :nt], gT[:, :, tsl], MUL, [dv, gT], [U_])

        def phaseGO(L, blks):
            mixin = tmp("mixin", [128, 8, NQ], BF16); hTall = tmp("hTall", [128, 8, NQ], BF16)
            for blk in blks:
                n, off = blk.n, blk.off
                r = rms_rstd([(xT[:, c, off:off + n], xT) for c in range(8)], n, 1024.0)
                for c in range(8):
                    stt(hTall[:, c, off:off + n], xT[:, c, off:off + n], pc(L, "g_pre", c), r[:, :n], MUL, MUL, [xT, prm[L], r], [hTall])
            wo = tmp("wo", [128, 8, 1024], BF16)
            for kc in range(8):
                P.dma("pool", wo[:, kc, :], Wout_d[L, :, kc, :], wo, writes=[wo])
            for e in range(8):
                wg = [wload("wgt%d" % nn, WinT_d[L, 38 + nn * 8 + e], [128, 8, 128], 2) for nn in range(3)]
                wb = [wload("wbr%d" % nn, Wbr_d[L, e, nn], [128, 4, 128], 2) for nn in range(3)]
                for blk in blks:
                    n, off = blk.n, blk.off
                    acc = tmp("gacc", [128, 512], F32, 2)
                    for nn in range(3):
                        bg = bank()
                        for kc in range(8):
                            mm(bg, bg[:, :n], wg[nn][:, kc, :], hTall[:, kc, off:off + n], kc == 0, kc == 7, [wg[nn], hTall])
                        sg = tmp("gsg", [128, 512], F32, 2)
                        act(sg[:, :n], bg[:, :n], AF.Sigmoid, [bg], [sg])
                        bu = bank()
                        for kc in range(4):
                            mm(bu, bu[:, :n], wb[nn][:, kc, :], brTv[nn][:, kc, off:off + n], kc == 0, kc == 3, [wb[nn], U_])
                        if nn == 0:
                            tt(acc[:, :n], sg[:, :n], bu[:, :n], MUL, [sg, bu], [acc])
                        else:
                            tt(sg[:, :n], sg[:, :n], bu[:, :n], MUL, [sg, bu], [sg])
                            if nn == 1:
                                tt(acc[:, :n], acc[:, :n], sg[:, :n], ADD, [acc, sg], [acc])
                            else:
                                tt(mixin[:, e, off:off + n], acc[:, :n], sg[:, :n], ADD, [acc, sg], [mixin])
            for blk in blks:
                n, off = blk.n, blk.off
                stage = tmp("stage", [128, 8, 512], F32, 1)
                for e in range(8):
                    b = bank()
                    for kc in range(8):
                        mm(b, b[:, :n], wo[:, kc, e * 128:(e + 1) * 128], mixin[:, kc, off:off + n], kc == 0, kc == 7, [wo, mixin])
                    cp(stage[:, e, :n], b[:, :n], [b], [stage])
                postnorm_residual(L, "g_postmix", blk, stage)

        def phaseF(L, blks):
            stageF = tmp("stageF", [128, 8, NQ], F32); h2 = tmp("h2", [128, 8, NQ], BF16)
            for blk in blks:
                n, off = blk.n, blk.off
                r = rms_rstd([(xT[:, c, off:off + n], xT) for c in range(8)], n, 1024.0)
                for c in range(8):
                    stt(h2[:, c, off:off + n], xT[:, c, off:off + n], pc(L, "g_preffn", c), r[:, :n], MUL, MUL, [xT, prm[L], r], [h2])
            NQa = NQ
            actv = lambda k: U_[:, 0:11 * NQa].rearrange("p (a b) -> p a b", b=NQa)[k]
            for half in range(2):
                for f in range(11):
                    wg = wload("wfg", Wg_d[L, half * 11 + f], [128, 8, 128], 5)
                    wu = wload("wfu", Wu_d[L, half * 11 + f], [128, 8, 128], 5)
                    for blk in blks:
                        n, off = blk.n, blk.off
                        bg = bank(); bu = bank()
                        for kc in range(8):
                            mm(bg, bg[:, :n], wg[:, kc, :], h2[:, kc, off:off + n], kc == 0, kc == 7, [wg, h2])
                        for kc in range(8):
                            mm(bu, bu[:, :n], wu[:, kc, :], h2[:, kc, off:off + n], kc == 0, kc == 7, [wu, h2])
                        sg = tmp("fsg", [128, 512], F32, 2)
                        act(sg[:, :n], bg[:, :n], AF.Silu, [bg], [sg])
                        tt(actv((slice(None), f, slice(off, off + n))), sg[:, :n], bu[:, :n], MUL, [sg, bu], [U_])
                for e in range(8):
                    wd = wload("wfd", Wd_d[L, e, half], [128, 11, 128], 4)
                    for blk in blks:
                        n, off = blk.n, blk.off
                        b = bank()
                        for kc in range(11):
                            mm(b, b[:, :n], wd[:, kc, :], actv((slice(None), kc, slice(off, off + n))), kc == 0, kc == 10, [wd, U_])
                        if half == 0:
                            cp(stageF[:, e, off:off + n], b[:, :n], [b], [stageF])
                        else:
                            tt(stageF[:, e, off:off + n], stageF[:, e, off:off + n], b[:, :n], ADD, [stageF, b], [stageF])
            for blk in blks:
                postnorm_residual(L, "g_postffn", blk, stageF, blk.off)

        def phaseP(L, blks):
            wpl = tmp("wpl", [128, 2, 1024], BF16); wpg = tmp("wpg", [128, 8, 1024], BF16)
            for kc in range(2):
                P.dma("pool", wpl[:, kc, :], Wple_d[L, :, kc, :], wpl, writes=[wpl])
            for kc in range(8):
                P.dma("pool", wpg[:, kc, :], Wpg_d[L, :, kc, :], wpg, writes=[wpg])
            for blk in blks:
                n, off = blk.n, blk.off
                xb = tmp("xb", [128, 8, 512], BF16, 1); pb = tmp("pb", [128, 2, 512], BF16, 2)
                cp(xb[:, :, :n], xT[:, :, off:off + n], [xT], [xb], eng="dve")
                for kc in range(2):
                    P.dma("pool", pb[:, kc, :n], pT_d[L, :, kc, blk.g0:blk.g0 + n], pb, writes=[pb])
                stage = tmp("stage", [128, 8, 512], F32, 1)
                for e in range(8):
                    es = slice(e * 128, (e + 1) * 128)
                    b1 = bank(); b2 = bank()
                    for kc in range(2):
                        mm(b1, b1[:, :n], wpl[:, kc, es], pb[:, kc, :n], kc == 0, kc == 1, [wpl, pb])
                    for kc in range(8):
                        mm(b2, b2[:, :n], wpg[:, kc, es], xb[:, kc, :n], kc == 0, kc == 7, [wpg, xb])
                    sg = tmp("psg", [128, 512], F32, 2)
                    act(sg[:, :n], b2[:, :n], AF.Sigmoid, [b2], [sg])
                    tt(stage[:, e, :n], sg[:, :n], b1[:, :n], MUL, [sg, b1], [stage])
                postnorm_residual(L, "g_ple", blk, stage)

        for H in range(2):
            g_base = H * 1024
            b512 = [Blk("p", 0, 512, g_base), Blk("p", 512, 512, g_base + 512)]
            b256 = [Blk("p", i * 256, 256, g_base + i * 256) for i in range(4)]
            if H == 1:
                b512.append(Blk("s", 1024, 64, 2048)); b256.append(Blk("s", 1024, 64, 2048))
            for blk in b512:
                for c in range(8):
                    P.dma("sp", xT[:, c, blk.off:blk.off + blk.n], xT_d[:, c, blk.g0:blk.g0 + blk.n], xT, writes=[xT])
            for L in range(NLAYERS):
                for bi_, blk in enumerate(b256):
                    lastp = (H == 1 and blk.kind == "p" and blk.off + blk.n == 1024)
                    arena_reset()
                    P.ph = "N"
                    hT = norm_to_bf16(L, "g_pre", blk)
                    if "A" in PHASES:
                        P.ph = "A"
                        phaseA(L, blk, hT, lastp)
                    if "B" in PHASES:
                        if blk.kind != "p":
                            arena_reset()
                        P.ph = "B"
                        phaseB(L, blk, hT, lastp)
                    if "C" in PHASES:
                        arena_reset()
                        if blk.kind == "p":
                            P.ph = "C"
                            cT_ = phaseC_proj(L, blk, hT, lastp)
                            for j in range(2):
                                sub = Blk("p", blk.off + j * 128, 128, blk.g0 + j * 128)
                                phaseC(L, sub, cT_, j * 128, lastp and j == 1)
                        else:
                            P.ph = "Cs"
                            cT_ = phaseC_proj(L, blk, hT, False)
                            phaseC(L, blk, cT_, 0, False)
                if DBG and H == 1 and L == 0:
                    arena_reset()
                    for i_ in range(3):
                        dt_ = tmp("dbgt", [128, 4, 64], F32, 3)
                        cp(dt_[:], brTv[i_][:, :, 1024:1088], [U_], [dt_], eng="dve")
                        P.dma("sp", dbg_d[i_], dt_[:], dt_, reads=[dt_])
                if "G" in PHASES:
                    arena_reset(); P.ph = "G"; phaseGO(L, b512)
                if "F" in PHASES:
                    arena_reset(); P.ph = "F"; phaseF(L, b512)
                if "P" in PHASES:
                    arena_reset(); P.ph = "P"; phaseP(L, b512)
            arena_reset()
            for blk in b512:
                for c in range(8):
                    P.dma("sp", yT_d[:, c, blk.g0:blk.g0 + blk.n], xT[:, c, blk.off:blk.off + blk.n], xT, reads=[xT])
        P.emit()
        build_program.stats = P.stats
        build_program.pe_ph = [o.get('ph', '') for o in P.ops if o.get('eng') == 'pe']
    return nc


_NC_CACHE = {}


def _cols(a):
    a = np.asarray(a, np.float32).reshape(-1, 128)
    return np.ascontiguousarray(a.T)


def kernel(**inp):
    f = lambda k: np.asarray(inp[k], np.float32)
    ca = np.ascontiguousarray
    prm = np.zeros((2, 128, NPRM), np.float32)
    def putp(L, name, arr):
        prm[L][:, PRM_OFF[name]:PRM_OFF[name] + arr.shape[1]] = arr
    for L in range(2):
        putp(L, "g_pre", _cols(f("norm_pre_mix")[L])); putp(L, "g_postmix", _cols(f("norm_post_mix")[L]))
        putp(L, "g_preffn", _cols(f("norm_pre_ffn")[L])); putp(L, "g_postffn", _cols(f("norm_post_ffn")[L]))
        putp(L, "g_ple", _cols(f("norm_ple")[L]))
        cw = f("conv_w")[L]
        putp(L, "conv_w", ca(cw.reshape(4, 4, 128).transpose(2, 1, 0).reshape(128, 16)))
        putp(L, "conv_b", _cols(f("conv_b")[L])); putp(L, "ba", _cols(f("lru_ba")[L])); putp(L, "bx", _cols(f("lru_bx")[L]))
        putp(L, "lam", _cols(f("lru_lambda")[L]))
        putp(L, "hlb", np.concatenate([_cols(f("hg_lower_bounds")[0]), _cols(f("hg_lower_bounds")[1])], axis=1))
        putp(L, "hg_norm_g", _cols(f("hg_norm_g")[L])); putp(L, "mu", _cols(f("rw_mu")[L])); putp(L, "w0", _cols(f("rw_w0")[L]))
        putp(L, "a0", _cols(f("rw_a0")[L])); putp(L, "k_k", _cols(f("rw_k_k")[L])); putp(L, "k_a", _cols(f("rw_k_a")[L]))
        putp(L, "r_k", _cols(f("rw_r_k")[L].reshape(-1))); putp(L, "ln_g", _cols(f("rw_ln_g")[L])); putp(L, "ln_b", _cols(f("rw_ln_b")[L]))
    w_in = f("w_in")
    WinT = ca(w_in.reshape(2, 8, 128, 62, 128).transpose(0, 3, 2, 1, 4))
    Wbi = ca(w_in[:, :, 2048:2560].reshape(2, 8, 128, 512).transpose(0, 2, 1, 3))
    def bd(w):
        o = np.zeros((2, 4, 128, 128), np.float32)
        for L in range(2):
            for c in range(4):
                o[L, c, 0:64, 0:64] = w[L, 2 * c]; o[L, c, 64:128, 64:128] = w[L, 2 * c + 1]
        return o
    shared = dict(
        prm=prm, cst=_make_cst(), WinT=WinT, Wbi=Wbi, WA=bd(f("lru_wa")), WX=bd(f("lru_wx")),
        wup=ca(f("rw_w_up")), aup=ca(f("rw_a_up")), gup=ca(f("rw_g_up")), rows=np.zeros((2, 2, 512), np.float32),
        Wbr=ca(f("w_branch").reshape(2, 3, 4, 128, 8, 128).transpose(0, 4, 1, 3, 2, 5)),
        Wout=ca(f("w_out").reshape(2, 8, 128, 1024).transpose(0, 2, 1, 3)),
        Wg=ca(f("w_ffn_gate").reshape(2, 8, 128, 22, 128).transpose(0, 3, 2, 1, 4)),
        Wu=ca(f("w_ffn_up").reshape(2, 8, 128, 22, 128).transpose(0, 3, 2, 1, 4)),
        Wd=ca(f("w_ffn_down").reshape(2, 2, 11, 128, 8, 128).transpose(0, 4, 1, 3, 2, 5)),
        Wple=ca(f("w_ple").reshape(2, 2, 128, 1024).transpose(0, 2, 1, 3)),
        Wpg=ca(f("w_ple_gate").reshape(2, 8, 128, 1024).transpose(0, 2, 1, 3)),
    )
    xp, xs, pp, psm = f("x_prompt"), f("x_sample"), f("p_prompt"), f("p_sample")
    sc, sl, sh_, sr, ss = f("state_conv_a"), f("state_lru_a"), f("state_hgrn"), f("state_rwkv"), f("state_shift_c")
    in_maps = []
    for c in range(8):
        n0 = 16 * c
        xt = np.concatenate([xp[c].T, xs[n0:n0 + 16].reshape(64, 1024).T], axis=1)
        pt = np.stack([np.concatenate([pp[L, c].T, psm[L, n0:n0 + 16].reshape(64, 256).T], axis=1) for L in range(2)])
        m = dict(shared)
        m.update(
            xT=ca(xt.reshape(8, 128, 2112).transpose(1, 0, 2)),
            pT=ca(pt.reshape(2, 2, 128, 2112).transpose(0, 2, 1, 3)),
            conv_in=ca(sc[:, n0:n0 + 16].reshape(2, 16, 3, 4, 128).transpose(0, 4, 3, 1, 2)),
            lru_in=ca(sl[:, n0:n0 + 16].reshape(2, 16, 4, 128).transpose(0, 3, 2, 1)),
            hg_in=ca(sh_[:, n0:n0 + 16]),
            rw_in=ca(sr[:, n0:n0 + 16].reshape(2, 16, 4, 2, 64, 64).transpose(0, 3, 5, 1, 2, 4).reshape(2, 128, 16, 4, 64)),
            sh_in=ca(ss[:, n0:n0 + 16].reshape(2, 16, 14, 128).transpose(0, 3, 2, 1)),
        )
        in_maps.append(m)
    if "nc" not in _NC_CACHE:
        _NC_CACHE["nc"] = build_program()
    nc = _NC_CACHE["nc"]
    res = run_bass_kernel_spmd(nc, in_maps, core_ids=list(range(8)))
    R = res.results
    y_p = np.zeros((8, 2048, 1024), np.float32); y_s = np.zeros((128, 4, 1024), np.float32)
    conv_p = np.zeros((2, 8, 3, 512), np.float32); conv_s = np.zeros((2, 128, 3, 512), np.float32)
    lru_p = np.zeros((2, 8, 512), np.float32); lru_s = np.zeros((2, 128, 512), np.float32)
    hg_p = np.zeros((2, 8, 4, 128, 128), np.float32); hg_s = np.zeros((2, 128, 4, 128, 128), np.float32)
    rw_p = np.zeros((2, 8, 8, 64, 64), np.float32); rw_s = np.zeros((2, 128, 8, 64, 64), np.float32)
    sh_p = np.zeros((2, 8, 1792), np.float32); sh_s = np.zeros((2, 128, 1792), np.float32)
    for c in range(8):
        r = R[c]; n0 = 16 * c
        y = np.asarray(r["yT"]).transpose(1, 0, 2).reshape(1024, 2112).T
        y_p[c] = y[:2048]; y_s[n0:n0 + 16] = y[2048:].reshape(16, 4, 1024)
        co = np.asarray(r["conv_o"]).transpose(0, 3, 4, 2, 1).reshape(2, 17, 3, 512)
        conv_p[:, c] = co[:, 0]; conv_s[:, n0:n0 + 16] = co[:, 1:]
        lo = np.asarray(r["lru_o"]).transpose(0, 3, 2, 1).reshape(2, 17, 512)
        lru_p[:, c] = lo[:, 0]; lru_s[:, n0:n0 + 16] = lo[:, 1:]
        ho = np.asarray(r["hg_o"])
        hg_p[:, c] = ho[:, 0]; hg_s[:, n0:n0 + 16] = ho[:, 1:]
        ro = np.asarray(r["rw_o"]).reshape(2, 2, 64, 17, 4, 64).transpose(0, 3, 4, 1, 5, 2).reshape(2, 17, 8, 64, 64)
        rw_p[:, c] = ro[:, 0]; rw_s[:, n0:n0 + 16] = ro[:, 1:]
        so = np.asarray(r["sh_o"]).transpose(0, 3, 2, 1).reshape(2, 17, 1792)
        sh_p[:, c] = so[:, 0]; sh_s[:, n0:n0 + 16] = so[:, 1:]
    if DBG:
        kernel.dbg = [np.asarray(R[c]["dbg_br"]) for c in range(8)]
    return (y_p, y_s, conv_p, lru_p, hg_p, rw_p, sh_p, conv_s, lru_s, hg_s, rw_s, sh_s)
```

```python
import numpy as np
import concourse.bass as bass
import concourse.mybir as mybir
from concourse.bass_utils import run_bass_kernel_spmd

F32 = mybir.dt.float32
BF16 = mybir.dt.bfloat16
AF = mybir.ActivationFunctionType
ALU = mybir.AluOpType
AX = mybir.AxisListType

ENGS = ("pe", "act", "dve", "pool", "sp")


class Tile:
    __slots__ = ("t", "name", "w", "r", "sem", "cnt", "space")

    def __init__(self, t, name, space):
        self.t = t
        self.name = name
        self.w = None
        self.r = []
        self.sem = None
        self.cnt = 0
        self.space = space

    def __getitem__(self, k):
        return self.t[k]


class Prog:
    def __init__(self, nc):
        self.nc = nc
        self.ops = []
        self.stack = None
        self.tiles = []
        self.dma_tiles = []
        self.psum_rot = 0
        self.free_semrefs = []

    def sb(self, name, shape, dt):
        t = self.stack.enter_context(self.nc.sbuf_tensor("sb_" + name, list(shape), dt))
        tl = Tile(t, name, "sb")
        self.tiles.append(tl)
        return tl

    def ps(self, name, shape, dt=F32):
        t = self.stack.enter_context(self.nc.psum_tensor("ps_" + name, list(shape), dt))
        tl = Tile(t, name, "ps")
        self.tiles.append(tl)
        return tl

    def _deps(self, eng, reads, writes):
        deps = []
        wset = set(id(t) for t in writes)
        for t in reads:
            if t.w is not None:
                deps.append(("raw", t.w))
        for t in writes:
            if t.w is not None:
                deps.append(("waw", t.w))
            for r in t.r:
                deps.append(("war", r))
        return deps

    def _commit(self, me, reads, writes):
        for t in reads:
            t.r.append(me)
        for t in writes:
            t.w = me
            t.r = []

    def op(self, eng, fn, reads=(), writes=()):
        reads = [t for t in reads if t is not None]
        writes = [t for t in writes if t is not None]
        oid = len(self.ops)
        deps = self._deps(eng, reads, writes)
        self.ops.append(dict(id=oid, eng=eng, fn=fn, deps=deps, dma=None, sig=False, sidx=None, ph=getattr(self, 'ph', '')))
        self._commit(("op", oid), reads, writes)
        return oid

    def dma(self, q, out_ap, in_ap, prim, reads=(), writes=(), **kw):
        reads = [t for t in reads if t is not None]
        writes = [t for t in writes if t is not None]
        oid = len(self.ops)
        deps = self._deps(q, reads, writes)
        if prim.sem is None:
            if self.free_semrefs:
                prim.sem = self.free_semrefs.pop()
            else:
                prim.sem = [self.stack.enter_context(self.nc.semaphore("d_%d" % len(self.dma_tiles))), 0]
                self.dma_tiles.append(prim.sem)
        prim.sem[1] += 16

        def fn(e, out_ap=out_ap, in_ap=in_ap, kw=kw):
            return e.dma_start(out=out_ap, in_=in_ap, **kw)

        self.ops.append(dict(id=oid, eng=q, fn=fn, deps=deps, dma=prim, dmasem=prim.sem, sig=False, sidx=None))
        self._commit(("dma", prim, oid), reads, writes)
        return oid

    def barrier(self):
        self.ops.append(dict(id=len(self.ops), eng=None, bar=True, deps=[], dma=None, sig=False, sidx=None))

    def emit(self):
        nc = self.nc
        ops = self.ops
        CE = ("pe", "act", "dve", "pool")
        lastop = {}
        for o in ops:
            if o.get("bar"):
                for e, i in lastop.items():
                    ops[i]["sig"] = True
                continue
            for kind, d in o["deps"]:
                if d[0] == "op":
                    p = ops[d[1]]
                    if p["eng"] == o["eng"] and o["dma"] is None:
                        if not (kind == "raw" and o["eng"] in ("act", "dve", "pool")):
                            continue
                    p["sig"] = True
            if o["dma"] is None:
                lastop[o["eng"]] = o["id"]
        cnt = {e: 0 for e in ENGS}
        for o in ops:
            if o.get("bar"):
                continue
            if o["dma"] is None and o["sig"]:
                cnt[o["eng"]] += 1
                o["sidx"] = cnt[o["eng"]]
        issued = {}
        tiles_by_id = {}
        lastsig = {e: 0 for e in CE}
        pending = {e: {} for e in ENGS}
        for o in ops:
            if o.get("bar"):
                snap = {}
                for e in CE:
                    if lastsig[e] > 0:
                        snap[("e", e)] = lastsig[e]
                for k, v in issued.items():
                    snap[("d", k)] = v
                for x in ENGS:
                    for k, v in snap.items():
                        if pending[x].get(k, 0) < v:
                            pending[x][k] = v
                continue
            w = {}
            for kind, d in o["deps"]:
                if d[0] == "op":
                    p = ops[d[1]]
                    if p["eng"] == o["eng"] and o["dma"] is None:
                        if not (kind == "raw" and o["eng"] in ("act", "dve", "pool")):
                            continue
                    k = ("e", p["eng"]); v = p["sidx"]
                else:
                    tile = d[1]
                    k = ("d", id(tile.sem)); v = issued.get(id(tile.sem), 0)
                    tiles_by_id[id(tile.sem)] = tile.sem
                if w.get(k, 0) < v:
                    w[k] = v
            pd = pending[o["eng"]]
            if pd and not o.get("nobar"):
                for k, v in pd.items():
                    if k == ("e", o["eng"]) and o["dma"] is None:
                        continue
                    if w.get(k, 0) < v:
                        w[k] = v
                pending[o["eng"]] = {}
            o["w2"] = list(w.items())
            if o["dma"] is not None:
                t = o["dmasem"]
                tiles_by_id[id(t)] = t
                issued[id(t)] = issued.get(id(t), 0) + 16
            elif o["sig"]:
                lastsig[o["eng"]] = o["sidx"]
        esem = {e: self.stack.enter_context(nc.semaphore("e_" + e)) for e in CE}
        self.final_waits = [(t[0], t[1]) for t in self.dma_tiles]
        byeng = {e: [o for o in ops if o.get("eng") == e] for e in ENGS}
        stats = {e: [len(byeng[e]), 0] for e in ENGS}

        def run(e, name):
            seen = {}
            for o in byeng[name]:
                for key, v in o["w2"]:
                    if key[0] == "e":
                        s = esem[key[1]]
                    else:
                        s = tiles_by_id[key[1]][0]
                    if seen.get(key, 0) >= v:
                        continue
                    seen[key] = v
                    e.wait_ge(s, v)
                    stats[name][1] += 1
                ins = o["fn"](e)
                if o["dma"] is not None:
                    ins.then_inc(o["dmasem"][0], 16)
                elif o["sig"]:
                    ins.then_inc(esem[name], 1)
            if name == "sp":
                for s, v in self.final_waits:
                    e.wait_ge(s, v)

        with nc.allow_non_contiguous_dma(reason="small strided state rows"), nc.Block() as block:
            @block.tensor
            def _(e):
                run(e, "pe")

            @block.scalar
            def _(e):
                run(e, "act")

            @block.vector
            def _(e):
                run(e, "dve")

            @block.gpsimd
            def _(e):
                run(e, "pool")

            @block.sync
            def _(e):
                run(e, "sp")
        self.stats = stats

NB = 256
NLAYERS = 2
PHASES = "ABCGFP"
CSTOP = 99
ALIAS_TRACK = True
DBG = False
INV = BF16
ARENA_W = 25452

_PRM = [("g_pre", 8), ("g_postmix", 8), ("g_preffn", 8), ("g_postffn", 8), ("g_ple", 8), ("conv_w", 16), ("conv_b", 4),
        ("ba", 4), ("bx", 4), ("lam", 4), ("hlb", 8), ("hg_norm_g", 4), ("mu", 14), ("w0", 4), ("a0", 4), ("k_k", 4),
        ("k_a", 4), ("r_k", 4), ("ln_g", 4), ("ln_b", 4)]
PRM_OFF = {}
_o = 0
for _n, _w in _PRM:
    PRM_OFF[_n] = _o; _o += _w
NPRM = _o
_CST = [("ident", 128), ("ones", 128), ("blk64", 128), ("blkmean", 128), ("identpair", 64), ("minc_p", 128), ("mstr_p", 128),
        ("mlow_p", 128), ("minc_s", 64), ("mstr_s", 64), ("mlow_s", 64), ("rmask_p", 256), ("rmask_s", 64), ("ind", 16), ("indp", 2)]
CST_OFF = {}
_o = 0
for _n, _w in _CST:
    CST_OFF[_n] = (_o, _o + _w); _o += _w
NCST = _o


def _make_cst():
    c = np.zeros((128, NCST), np.float32)
    def put(name, arr):
        a, b = CST_OFF[name]
        c[:arr.shape[0], a:a + arr.shape[1]] = arr
    p = np.arange(128)
    put("ident", np.eye(128, dtype=np.float32)); put("ones", np.ones((128, 128), np.float32))
    b64 = (p[:, None] // 64 == p[None, :] // 64).astype(np.float32)
    put("blk64", b64); put("blkmean", b64 / 64.0)
    put("identpair", (p[:, None] % 64 == np.arange(64)[None, :]).astype(np.float32))
    for sfx, n, C in (("p", 128, 64), ("s", 64, 4)):
        s = np.arange(n)[:, None]; t = np.arange(n)[None, :]
        same = (s // C == t // C)
        put("minc_" + sfx, (same & (s <= t)).astype(np.float32))
        put("mstr_" + sfx, (same & (s < t)).astype(np.float32))
        put("mlow_" + sfx, (same & (t < s)).astype(np.float32))
    put("rmask_p", np.tile((np.arange(256) % 64 != 0).astype(np.float32)[None, :], (128, 1)))
    put("rmask_s", np.tile((np.arange(64) % 4 != 0).astype(np.float32)[None, :], (128, 1)))
    put("ind", (np.arange(128)[:, None] // 4 == np.arange(16)[None, :]).astype(np.float32))
    put("indp", (np.arange(128)[:, None] // 64 == np.arange(2)[None, :]).astype(np.float32))
    return c

from contextlib import ExitStack

EPS = 1e-6
GN_EPS = 64e-5
DEC_C = 0.6065306597126334
NQ = 1088


class Blk:
    def __init__(self, kind, off, n, g0):
        self.kind, self.off, self.n, self.g0 = kind, off, n, g0
        if kind == "p":
            self.G, self.Tg, self.C = 1, n, 64
            self.tiles = [(t * 128, 128, [(0, 64), (64, 64)]) for t in range(n // 128)]
        else:
            self.G, self.Tg, self.C = 16, 4, 4
            self.tiles = [(0, 64, [(4 * i, 4) for i in range(16)])]
        self.nch = n // self.C


def build_program(dbg=None):
    nc = bass.Bass("TRN2", target_bir_lowering=False)
    din = lambda name, shape: nc.dram_tensor(name, list(shape), F32, kind="ExternalInput").ap()
    dout = lambda name, shape: nc.dram_tensor(name, list(shape), F32, kind="ExternalOutput").ap()
    xT_d = din("xT", [128, 8, 2112]); pT_d = din("pT", [2, 128, 2, 2112])
    conv_in = din("conv_in", [2, 128, 4, 16, 3]); lru_in = din("lru_in", [2, 128, 4, 16])
    hg_in = din("hg_in", [2, 16, 4, 128, 128]); rw_in = din("rw_in", [2, 128, 16, 4, 64])
    sh_in = din("sh_in", [2, 128, 14, 16])
    prm_d = din("prm", [2, 128, NPRM]); cst_d = din("cst", [128, NCST])
    WinT_d = din("WinT", [2, 62, 128, 8, 128]); Wbi_d = din("Wbi", [2, 128, 8, 512])
    WA_d = din("WA", [2, 4, 128, 128]); WX_d = din("WX", [2, 4, 128, 128])
    wup_d = din("wup", [2, 64, 512]); aup_d = din("aup", [2, 64, 512]); gup_d = din("gup", [2, 128, 512])
    rows_d = din("rows", [2, 2, 512])
    Wbr_d = din("Wbr", [2, 8, 3, 128, 4, 128]); Wout_d = din("Wout", [2, 128, 8, 1024])
    Wg_d = din("Wg", [2, 22, 128, 8, 128]); Wu_d = din("Wu", [2, 22, 128, 8, 128])
    Wd_d = din("Wd", [2, 8, 2, 128, 11, 128]); Wple_d = din("Wple", [2, 128, 2, 1024]); Wpg_d = din("Wpg", [2, 128, 8, 1024])
    yT_d = dout("yT", [128, 8, 2112])
    conv_o = dout("conv_o", [2, 128, 4, 17, 3]); lru_o = dout("lru_o", [2, 128, 4, 17])
    hg_o = dout("hg_o", [2, 17, 4, 128, 128]); rw_o = dout("rw_o", [2, 128, 17, 4, 64])
    sh_o = dout("sh_o", [2, 128, 14, 17])
    dbg_d = dout("dbg_br", [3, 128, 4, 64]) if DBG else None

    with ExitStack() as st:
        P = Prog(nc); P.stack = st
        tmpd = {}

        def ptmp(tag, shape, dt, bufs=2):
            key = (tag, tuple(shape), dt)
            if key not in tmpd:
                tmpd[key] = [[P.sb("%s_%d_%d" % (tag, len(tmpd), i), shape, dt) for i in range(bufs)], 0]
            e = tmpd[key]; t = e[0][e[1] % bufs]; e[1] += 1
            return t

        ARW = ARENA_W
        arena_raw = st.enter_context(nc.sbuf_tensor("arena", [128, ARW], F32))
        ar = {"off": 0, "tiles": {}, "n": 0}

        ar["map"] = []

        def arena_reset():
            if not ALIAS_TRACK:
                P.barrier()
            ar["off"] = 0
            for lst, _ in ar["tiles"].values():
                for tl in lst:
                    if tl.sem is not None:
                        P.free_semrefs.append(tl.sem)
            ar["tiles"] = {}

        def tmp(tag, shape, dt, bufs=1):
            if tag not in ar["tiles"]:
                lst = []
                nel = 1
                for s_ in shape[1:]:
                    nel *= s_
                words = (nel * (2 if dt == BF16 else 4) + 3) // 4
                for i in range(bufs):
                    assert ar["off"] + words <= ARW, ("arena overflow", tag, ar["off"], words)
                    v = arena_raw[:, ar["off"]:ar["off"] + words]; ar["off"] += words
                    if dt == BF16:
                        v = v.bitcast(BF16)[:, :nel]
                    if len(shape) == 3:
                        v = v.rearrange("p (a b) -> p a b", b=shape[2])
                    elif len(shape) == 4:
                        v = v.rearrange("p (a b c) -> p a b c", b=shape[2], c=shape[3])
                    ar["n"] += 1
                    tl = Tile(v, "%s_%d" % (tag, ar["n"]), "sb"); P.tiles.append(tl)
                    if ALIAS_TRACK:
                        a1 = ar["off"] - words; b1 = ar["off"]
                        nm = []
                        for (a0, b0, t0) in ar["map"]:
                            if a0 < b1 and b0 > a1:
                                if t0.w is not None:
                                    tl.r.append(t0.w)
                                tl.r.extend(t0.r)
                                if a0 >= a1 and b0 <= b1:
                                    continue
                            nm.append((a0, b0, t0))
                        nm.append((a1, b1, tl)); ar["map"] = nm
                    lst.append(tl)
                ar["tiles"][tag] = [lst, 0]
            e = ar["tiles"][tag]; t = e[0][e[1] % len(e[0])]; e[1] += 1
            return t

        banks = [P.ps("bank%d" % i, [128, 512], F32) for i in range(6)]
        bank_o2 = P.ps("bank_o2", [128, 512], F32)
        bankb = P.ps("bankb", [128, 1024], BF16)
        brot = [0]

        def bank():
            b = banks[brot[0] % 6]; brot[0] += 1
            return b

        def mm(ps, out, lhsT, rhs, start, stop, reads):
            P.op("pe", lambda e: e.matmul(out, lhsT=lhsT, rhs=rhs, start=start, stop=stop), reads, [ps])

        def tr(ps, out, in_, ident, reads):
            P.op("pe", lambda e: e.transpose(out, in_, ident), reads, [ps])

        def act(out, in_, func, reads, writes, scale=1.0, bias=0.0):
            P.op("act", lambda e: e.activation(out=out, in_=in_, func=func, scale=scale, bias=bias), reads, writes)

        def tt(out, a, b, op, reads, writes, eng="dve"):
            P.op(eng, lambda e: e.tensor_tensor(out=out, in0=a, in1=b, op=op), reads, writes)

        def ts(out, a, s1, s2, op0, op1, reads, writes, eng="dve"):
            if s2 is None:
                P.op(eng, lambda e: e.tensor_scalar(out=out, in0=a, scalar1=s1, scalar2=None, op0=op0), reads, writes)
            else:
                P.op(eng, lambda e: e.tensor_scalar(out=out, in0=a, scalar1=s1, scalar2=s2, op0=op0, op1=op1), reads, writes)

        def stt(out, a, s, b, op0, op1, reads, writes):
            P.op("dve", lambda e: e.scalar_tensor_tensor(out=out, in0=a, scalar=s, in1=b, op0=op0, op1=op1), reads, writes)

        def cp(out, in_, reads, writes, eng="act"):
            if eng == "act":
                P.op("act", lambda e: e.copy(out=out, in_=in_), reads, writes)
            else:
                P.op(eng, lambda e: e.tensor_copy(out=out, in_=in_), reads, writes)

        def recip(out, in_, reads, writes):
            P.op("dve", lambda e: e.reciprocal(out=out, in_=in_), reads, writes)

        def memset(ap, v, writes, eng="dve"):
            P.op(eng, lambda e: e.memset(ap, v), [], writes)

        def scan(out, d0, d1, reads, writes):
            P.op("dve", lambda e: e.tensor_tensor_scan(out=out, data0=d0, data1=d1, initial=0.0, op0=ALU.mult, op1=ALU.add), reads, writes)

        MUL, ADD, SUB, MAX = ALU.mult, ALU.add, ALU.subtract, ALU.max

        cst = P.sb("cst", [128, NCST], F32)
        P.dma("sp", cst[:], cst_d, cst, writes=[cst])
        cstb = P.sb("cstb", [128, 384], BF16)
        cp(cstb[:], cst[:, 0:384], [cst], [cstb], eng="dve")
        C_ = lambda name, bf=False: (cstb if bf else cst)[:, CST_OFF[name][0]:CST_OFF[name][1]]
        identf = C_("ident"); identb = C_("ident", True); onesb = C_("ones", True)
        blk64b = C_("blk64", True); blkmean = C_("blkmean"); identpair = C_("identpair")

        prm = [P.sb("prm%d" % L, [128, NPRM], F32) for L in range(2)]
        for L in range(2):
            P.dma("sp", prm[L][:], prm_d[L], prm[L], writes=[prm[L]])
        der = [P.sb("der%d" % L, [128, 24], F32) for L in range(2)]

        def pc(L, name, i=0):
            o = PRM_OFF[name] + i
            return prm[L][:, o:o + 1]

        for L in range(2):
            lam = prm[L][:, PRM_OFF["lam"]:PRM_OFF["lam"] + 4]
            act(der[L][:, 0:4], lam, AF.Exp, [prm[L]], [der[L]], scale=-1.0)
            act(der[L][:, 0:4], der[L][:, 0:4], AF.Ln, [der[L]], [der[L]], bias=1.0)
            ts(der[L][:, 4:8], der[L][:, 0:4], -16.0, None, MUL, None, [der[L]], [der[L]])
            ts(der[L][:, 8:12], der[L][:, 0:4], 8.0, None, MUL, None, [der[L]], [der[L]])
            ts(der[L][:, 0:4], der[L][:, 0:4], -8.0, None, MUL, None, [der[L]], [der[L]])
            if L == 0:
                memset(der[L][:, 12:16], 0.0, [der[L]])
            else:
                l0 = prm[L][:, PRM_OFF["hlb"]:PRM_OFF["hlb"] + 4]; l1 = prm[L][:, PRM_OFF["hlb"] + 4:PRM_OFF["hlb"] + 8]
                tt(der[L][:, 12:16], l1, l0, SUB, [prm[L]], [der[L]])
                act(der[L][:, 12:16], der[L][:, 12:16], AF.Sigmoid, [der[L]], [der[L]])
            ts(der[L][:, 16:20], der[L][:, 12:16], -1.0, 1.0, MUL, ADD, [der[L]], [der[L]])
            ka = prm[L][:, PRM_OFF["k_a"]:PRM_OFF["k_a"] + 4]
            ts(der[L][:, 20:24], ka, -1.0, 1.0, MUL, ADD, [prm[L]], [der[L]])
        dc = lambda L, o: der[L][:, o:o + 1]

        xT = P.sb("xT", [128, 8, NQ], F32)
        U_ = P.sb("U_", [128, 12 * NQ], BF16)
        class _V:
            def __init__(s_, i): s_.i = i
            def __getitem__(s_, k): return U_[:, s_.i * 4 * NQ:(s_.i + 1) * 4 * NQ].rearrange("p (a b) -> p a b", b=NQ)[k]
        brTv = [_V(i) for i in range(3)]
        hTp = P.sb("hTp", [128, 8, 512], BF16)
        convtail = [P.sb("ctail%d" % L, [128, 4, 3], F32) for L in range(2)]
        lruh = [P.sb("lruh%d" % L, [128, 4], F32) for L in range(2)]
        shprev = [P.sb("shprev%d" % L, [128, 14], F32) for L in range(2)]
        hgS = {}; rwM = {}
        for L in range(2):
            memset(convtail[L][:], 0.0, [convtail[L]]); memset(lruh[L][:], 0.0, [lruh[L]]); memset(shprev[L][:], 0.0, [shprev[L]])
            s = ptmp("hgS%d" % L, [128, 4, 128], F32, 2); sb_ = ptmp("hgSb%d" % L, [128, 4, 128], BF16, 3)
            memset(s[:], 0.0, [s]); memset(sb_[:], 0.0, [sb_]); hgS[L] = (s, sb_)
            m = ptmp("rwM%d" % L, [128, 4, 64], F32, 2); mb = ptmp("rwMbz%d" % L, [128, 8, 64], BF16, 3)
            memset(m[:], 0.0, [m]); memset(mb[:], 0.0, [mb]); rwM[L] = (m, mb)

        def wload(tag, src, shape, bufs=3):
            w = tmp(tag, shape, BF16, bufs)
            P.dma("pool", w[:], src, w, writes=[w])
            return w

        def rms_rstd(srcs, n, D):
            b = bank()
            for c, (ap, t) in enumerate(srcs):
                sq = tmp("sq", [128, 512], BF16, 3)
                act(sq[:, :n], ap, AF.Square, [t], [sq])
                mm(b, b[:, :n], onesb, sq[:, :n], c == 0, c == len(srcs) - 1, [cstb, sq])
            r = tmp("rstd", [128, 512], F32, 2)
            act(r[:, :n], b[:, :n], AF.Ln, [b], [r], scale=1.0 / D, bias=EPS)
            act(r[:, :n], r[:, :n], AF.Exp, [r], [r], scale=-0.5)
            return r

        def norm_to_bf16(L, gname, blk):
            n, off = blk.n, blk.off
            r = rms_rstd([(xT[:, c, off:off + n], xT) for c in range(8)], n, 1024.0)
            h = hTp
            for c in range(8):
                stt(h[:, c, :n], xT[:, c, off:off + n], pc(L, gname, c), r[:, :n], MUL, MUL, [xT, prm[L], r], [h])
            return h

        def postnorm_residual(L, gname, blk, stage, so=0):
            n, off = blk.n, blk.off
            r = rms_rstd([(stage[:, c, so:so + n], stage) for c in range(8)], n, 1024.0)
            for c in range(8):
                stt(stage[:, c, so:so + n], stage[:, c, so:so + n], pc(L, gname, c), r[:, :n], MUL, MUL, [stage, prm[L], r], [stage])
            tt(xT[:, :, off:off + n], xT[:, :, off:off + n], stage[:, :, so:so + n], ADD, [xT, stage], [xT])

        win_bufs = [P.sb("winp%d" % i, [128, 8, 128], BF16) for i in range(4)]
        win_rot = [0]

        def proj_fm(L, col, hT, n, ho=0):
            w = win_bufs[win_rot[0] % 4]; win_rot[0] += 1
            oid_ = P.dma("pool", w[:], WinT_d[L, col], w, writes=[w])
            P.ops[oid_]["nobar"] = True
            b = bank()
            for kc in range(8):
                mm(b, b[:, :n], w[:, kc, :], hT[:, kc, ho:ho + n], kc == 0, kc == 7, [w, hT])
            return b

        def phaseA(L, blk, hT, last):
            G, Tg, n, off = blk.G, blk.Tg, blk.n, blk.off
            sfx = blk.kind
            xcat = tmp("xcat" + sfx, [128, 4, G, 3 + Tg], F32, 1)
            if sfx == "p":
                cp(xcat[:, :, 0, 0:3], convtail[L][:], [convtail[L]], [xcat], eng="dve")
            else:
                for c in range(4):
                    P.dma("sp", xcat[:, c, :, 0:3], conv_in[L, :, c], xcat, writes=[xcat])
            gg = tmp("gg", [128, 4, NB], F32, 1)
            for c in range(8):
                b = proj_fm(L, c, hT, n)
                if c < 4:
                    cp(xcat[:, c, :, 3:3 + Tg], b[:, :n].rearrange("p (g t) -> p g t", t=Tg), [b], [xcat])
                else:
                    u = tmp("ga_u", [128, NB], F32, 2)
                    act(u[:, :n], b[:, :n], AF.Square, [b], [u])
                    ts(u[:, :n], u[:, :n], 0.044715, 1.0, MUL, ADD, [u], [u])
                    tt(u[:, :n], u[:, :n], b[:, :n], MUL, [u, b], [u])
                    act(u[:, :n], u[:, :n], AF.Sigmoid, [u], [u], scale=1.5957691216057308)
                    tt(gg[:, c - 4, :n], u[:, :n], b[:, :n], MUL, [u, b], [gg])
            xc = tmp("xc" + sfx, [128, 4, G, Tg], F32, 1); xcb = tmp("xcb" + sfx, [128, 4, G, Tg], BF16, 1)
            for c in range(4):
                ts(xc[:, c], xcat[:, c, :, 0:Tg], pc(L, "conv_w", c * 4), pc(L, "conv_b", c), MUL, ADD, [xcat, prm[L]], [xc])
                for j in range(1, 4):
                    stt(xc[:, c], xcat[:, c, :, j:j + Tg], pc(L, "conv_w", c * 4 + j), xc[:, c], MUL, ADD, [xcat, prm[L], xc], [xc])
            cp(xcb[:], xc[:], [xc], [xcb])
            if sfx == "p":
                cp(convtail[L][:], xcat[:, :, 0, Tg:Tg + 3], [xcat], [convtail[L]], eng="dve")
                if last:
                    P.dma("sp", conv_o[L, :, :, 0, :], convtail[L][:], convtail[L], reads=[convtail[L]])
            else:
                for c in range(4):
                    P.dma("sp", conv_o[L, :, c, 1:17, :], xcat[:, c, :, Tg:Tg + 3], xcat, reads=[xcat])
            A1 = tmp("A1" + sfx, [128, 4, G, Tg + 1], F32, 1); B1 = tmp("B1" + sfx, [128, 4, G, Tg + 1], F32, 1)
            memset(A1[:, :, :, 0:1], 0.0, [A1])
            if sfx == "p":
                cp(B1[:, :, 0, 0:1], lruh[L][:].unsqueeze(2), [lruh[L]], [B1], eng="dve")
            else:
                lrs = tmp("lrs", [128, 4, 16], F32, 1)
                P.dma("sp", lrs[:], lru_in[L], lrs, writes=[lrs])
                cp(B1[:, :, :, 0], lrs[:], [lrs], [B1], eng="dve")
            wa = wload("wa", WA_d[L].rearrange("c p m -> p c m"), [128, 4, 128], 2)
            wx = wload("wx", WX_d[L].rearrange("c p m -> p c m"), [128, 4, 128], 2)
            v3 = lambda t_: t_[:, :n].rearrange("p (g t) -> p g t", t=Tg)
            for c in range(4):
                xcbf = xcb[:, c].rearrange("p g t -> p (g t)")
                b1 = bank(); mm(b1, b1[:, :n], wa[:, c, :], xcbf, True, True, [wa, xcb])
                rt = tmp("rt", [128, NB], F32, 2)
                act(rt[:, :n], b1[:, :n], AF.Sigmoid, [b1, prm[L]], [rt], bias=pc(L, "ba", c))
                b2 = bank(); mm(b2, b2[:, :n], wx[:, c, :], xcbf, True, True, [wx, xcb])
                it = tmp("it", [128, NB], F32, 2)
                act(it[:, :n], b2[:, :n], AF.Sigmoid, [b2, prm[L]], [it], bias=pc(L, "bx", c))
                act(A1[:, c, :, 1:], v3(rt), AF.Exp, [rt, der[L]], [A1], scale=dc(L, 0 + c))
                t1 = tmp("lt1", [128, NB], F32, 2); t2 = tmp("lt2", [128, NB], F32, 2)
                act(t1[:, :n], rt[:, :n], AF.Exp, [rt, der[L]], [t1], scale=dc(L, 4 + c))
                act(t2[:, :n], rt[:, :n], AF.Tanh, [rt, der[L]], [t2], scale=dc(L, 8 + c))
                stt(t1[:, :n], t1[:, :n], 1.0, t2[:, :n], ADD, MUL, [t1, t2], [t1])
                act(t1[:, :n], t1[:, :n], AF.Sqrt, [t1], [t1])
                if sfx == "p" and blk.g0 == 0:
                    memset(t1[:, 0:1], 1.0, [t1])
                tt(t2[:, :n], xc[:, c].rearrange("p g t -> p (g t)"), it[:, :n], MUL, [xc, it], [t2])
                tt(B1[:, c, :, 1:], v3(t2), v3(t1), MUL, [t1, t2], [B1])
            H1 = tmp("H1" + sfx, [128, 4, G, Tg + 1], F32, 1)
            for c in range(4):
                fl = lambda t_: t_[:, c].rearrange("p g t -> p (g t)")
                scan(fl(H1), fl(A1), fl(B1), [A1, B1], [H1])
            for c in range(4):
                tt(brTv[0][:, c, off:off + n].rearrange("p (g t) -> p g t", t=Tg), H1[:, c, :, 1:],
                   gg[:, c, :n].rearrange("p (g t) -> p g t", t=Tg), MUL, [H1, gg], [U_])
            if sfx == "p":
                cp(lruh[L][:].unsqueeze(2), H1[:, :, 0, Tg:Tg + 1], [H1], [lruh[L]], eng="dve")
                if last:
                    P.dma("sp", lru_o[L, :, :, 0], lruh[L][:], lruh[L], reads=[lruh[L]])
            else:
                lro = tmp("lro", [128, 4, 16], F32, 1)
                cp(lro[:], H1[:, :, :, Tg], [H1], [lro], eng="dve")
                P.dma("sp", lru_o[L, :, :, 1:17], lro[:], lro, reads=[lro])

        def phaseB(L, blk, hT, last):
            G, Tg, n, off, C, nch = blk.G, blk.Tg, blk.n, blk.off, blk.C, blk.nch
            sfx = blk.kind
            rmask = C_("rmask_" + sfx)[:, :n]
            qs = tmp("hq", [128, 4, NB], F32, 1); ff = tmp("hf", [128, 4, NB], F32, 1)
            kk = tmp("hk", [128, 4, NB], F32, 1); gs = tmp("hg", [128, 4, NB], F32, 1)
            bc = tmp("hbc", [128, 4, NB], F32, 1); ebc = tmp("hebc", [128, 4, NB], F32, 1)
            for h in range(4):
                b = proj_fm(L, 8 + h, hT, n)
                act(qs[:, h, :n], b[:, :n], AF.Silu, [b], [qs])
            for h in range(4):
                b = proj_fm(L, 12 + h, hT, n)
                act(ff[:, h, :n], b[:, :n], AF.Sigmoid, [b], [ff])
                ts(ff[:, h, :n], ff[:, h, :n], dc(L, 16 + h), dc(L, 12 + h), MUL, ADD, [ff, der[L]], [ff])
                ts(ff[:, h, :n], ff[:, h, :n], 1e-20, None, MAX, None, [ff], [ff])
                ts(kk[:, h, :n], ff[:, h, :n], -1.0, 1.0, MUL, ADD, [ff], [kk])
                act(ff[:, h, :n], ff[:, h, :n], AF.Ln, [ff], [ff])
                scan(bc[:, h, :n], rmask, ff[:, h, :n], [cst, ff], [bc])
            for h in range(4):
                b = proj_fm(L, 20 + h, hT, n)
                act(gs[:, h, :n], b[:, :n], AF.Silu, [b], [gs])
            qt = tmp("hqt", [128, 4, NB], BF16, 1); kt = tmp("hkt", [128, 4, NB], BF16, 1); kd = tmp("hkd", [128, 4, NB], BF16, 1)
            act(ebc[:, :, :n], bc[:, :, :n], AF.Exp, [bc], [ebc])
            tt(qt[:, :, :n], qs[:, :, :n], ebc[:, :, :n], MUL, [qs, ebc], [qt])
            e2 = tmp("he2", [128, 4, NB], F32, 1)
            act(e2[:, :, :n], bc[:, :, :n], AF.Exp, [bc], [e2], scale=-1.0)
            tt(kt[:, :, :n], kk[:, :, :n], e2[:, :, :n], MUL, [kk, e2], [kt])
            for h in range(4):
                bv = bc[:, h, :n].rearrange("p (c t) -> p c t", t=C)
                tt(e2[:, h, :n].rearrange("p (c t) -> p c t", t=C), bv[:, :, C - 1:C].to_broadcast([128, nch, C]), bv, SUB, [bc], [e2])
            act(e2[:, :, :n], e2[:, :, :n], AF.Exp, [e2], [e2])
            tt(kd[:, :, :n], kk[:, :, :n], e2[:, :, :n], MUL, [kk, e2], [kd])
            wbi = wload("wbi", Wbi_d[L], [128, 8, 512], 1)
            for (toff, nt, chunks) in blk.tiles:
                tsl = slice(toff, toff + nt)
                b = bank()
                for kc in range(8):
                    mm(b, b[:nt, :], hT[:, kc, tsl], wbi[:, kc, :], kc == 0, kc == 7, [hT, wbi])
                vt = tmp("hvt", [128, 512], BF16, 2)
                cp(vt[:nt, :], b[:nt, :], [b], [vt])
                for h in range(4):
                    tr(bankb, bankb[:nt, h * 128:(h + 1) * 128], kd[:, h, tsl], identb, [kd, cstb])
                kdt = tmp("hkdt", [128, 512], BF16, 2)
                cp(kdt[:nt, :], bankb[:nt, 0:512], [bankb], [kdt], eng="dve")
                Sb_list = []
                if sfx == "p":
                    S, Sb = hgS[L]
                    for (r0, Cc) in chunks:
                        Sb_list.append(Sb)
                        bs = bank()
                        for h in range(4):
                            mm(bs, bs[:, h * 128:(h + 1) * 128], kdt[r0:r0 + Cc, h * 128:(h + 1) * 128], vt[r0:r0 + Cc, h * 128:(h + 1) * 128], True, True, [kdt, vt])
                        S2 = ptmp("hgS%d" % L, [128, 4, 128], F32, 2); Sb2 = ptmp("hgSb%d" % L, [128, 4, 128], BF16, 3)
                        col = toff + r0 + Cc - 1
                        for h in range(4):
                            stt(S2[:, h, :], S[:, h, :], ebc[:, h, col:col + 1], bs[:, h * 128:(h + 1) * 128], MUL, ADD, [S, ebc, bs], [S2])
                        cp(Sb2[:], S2[:], [S2], [Sb2])
                        S, Sb = S2, Sb2
                    hgS[L] = (S, Sb)
                    if last and toff + nt == n:
                        P.dma("sp", hg_o[L, 0].rearrange("h k v -> k h v"), S[:], S, reads=[S])
                else:
                    for i, (r0, Cc) in enumerate(chunks):
                        if i == 0:
                            nxtS = tmp("hSin", [128, 4, 128], F32, 3)
                            P.dma("sp", nxtS[:], hg_in[L, 0].rearrange("h k v -> k h v"), nxtS, writes=[nxtS])
                        Sin = nxtS
                        if i + 1 < len(chunks):
                            nxtS = tmp("hSin", [128, 4, 128], F32, 3)
                            P.dma("sp", nxtS[:], hg_in[L, i + 1].rearrange("h k v -> k h v"), nxtS, writes=[nxtS])
                        Sbi = tmp("hSbi", [128, 4, 128], BF16, 16)
                        cp(Sbi[:], Sin[:], [Sin], [Sbi])
                        Sb_list.append(Sbi)
                        kdm = tmp("hkdm", [128, 512], BF16, 3)
                        ts(kdm[:nt, :], kdt[:nt, :], C_("ind")[:nt, i:i + 1], None, MUL, None, [kdt, cst], [kdm])
                        bs = bank()
                        for h in range(4):
                            mm(bs, bs[:, h * 128:(h + 1) * 128], kdm[:nt, h * 128:(h + 1) * 128], vt[:nt, h * 128:(h + 1) * 128], True, True, [kdm, vt])
                        So = tmp("hSo", [128, 4, 128], F32, 3)
                        col = toff + r0 + Cc - 1
                        for h in range(4):
                            stt(So[:, h, :], Sin[:, h, :], ebc[:, h, col:col + 1], bs[:, h * 128:(h + 1) * 128], MUL, ADD, [Sin, ebc, bs], [So])
                        P.dma("sp", hg_o[L, 1 + i].rearrange("h k v -> k h v"), So[:], So, reads=[So])
                ba = bank()
                for h in range(4):
                    mm(ba, ba[:nt, h * 128:h * 128 + nt], kt[:, h, tsl], qt[:, h, tsl], True, True, [kt, qt])
                At = tmp("hAt", [128, 4, 128], BF16, 2)
                mk = C_("minc_" + sfx)[:nt, :nt]
                tt(At[:nt, :, :nt], ba[:nt, :].rearrange("p (h t) -> p h t", t=128)[:, :, :nt], mk.unsqueeze(1).to_broadcast([nt, 4, nt]), MUL, [ba, cst], [At])
                bo = bank()
                for h in range(4):
                    mm(bo, bo[:, h * 128:h * 128 + nt], vt[:nt, h * 128:(h + 1) * 128], At[:nt, h, :nt], True, False, [vt, At])
                    for i, (r0, Cc) in enumerate(chunks):
                        mm(bo, bo[:, h * 128 + r0:h * 128 + r0 + Cc], Sb_list[i][:, h, :], qt[:, h, toff + r0:toff + r0 + Cc], False, i == len(chunks) - 1, [Sb_list[i], qt])
                bo3 = bo[:, :].rearrange("p (h t) -> p h t", t=128)[:, :, :nt]
                sq = tmp("hsq", [128, 4, 128], BF16, 1)
                act(sq[:, :, :nt], bo3, AF.Square, [bo], [sq])
                bn = bank()
                for h in range(4):
                    mm(bn, bn[:, h * 128:h * 128 + nt], onesb, sq[:, h, :nt], True, True, [cstb, sq])
                bn3 = bn[:, :].rearrange("p (h t) -> p h t", t=128)[:, :, :nt]
                rs = tmp("hrs", [128, 4, 128], F32, 1)
                act(rs[:, :, :nt], bn3, AF.Ln, [bn], [rs], scale=1.0 / 128, bias=EPS)
                act(rs[:, :, :nt], rs[:, :, :nt], AF.Exp, [rs], [rs], scale=-0.5)
                tt(rs[:, :, :nt], bo3, rs[:, :, :nt], MUL, [bo, rs], [rs])
                for h in range(4):
                    stt(brTv[1][:, h, off + toff:off + toff + nt], rs[:, h, :nt], pc(L, "hg_norm_g", h), gs[:, h, tsl], MUL, MUL, [rs, prm[L], gs], [U_])

        def phaseC_proj(L, blk, hT, last):
            G, Tg, n = blk.G, blk.Tg, blk.n
            sfx = blk.kind
            cT = tmp("cT" + sfx, [128, 14, G, Tg + 1], F32, 1)
            if sfx == "p":
                cp(cT[:, :, 0, 0:1], shprev[L][:].unsqueeze(2), [shprev[L]], [cT], eng="dve")
            else:
                shs = tmp("shs", [128, 14, 16], F32, 1)
                P.dma("sp", shs[:], sh_in[L], shs, writes=[shs])
                cp(cT[:, :, :, 0], shs[:], [shs], [cT], eng="dve")
            for c in range(14):
                b = proj_fm(L, 24 + c, hT, n, 0)
                cp(cT[:, c, :, 1:], b[:, :n].rearrange("p (g t) -> p g t", t=Tg), [b], [cT])
            if sfx == "p":
                cp(shprev[L][:].unsqueeze(2), cT[:, :, 0, Tg:Tg + 1], [cT], [shprev[L]], eng="dve")
                if last:
                    P.dma("sp", sh_o[L, :, :, 0], shprev[L][:], shprev[L], reads=[shprev[L]])
            else:
                sho = tmp("sho", [128, 14, 16], F32, 1)
                cp(sho[:], cT[:, :, :, Tg], [cT], [sho], eng="dve")
                P.dma("sp", sh_o[L, :, :, 1:17], sho[:], sho, reads=[sho])
            return cT

        def phaseC(L, blk, cT, c0, last):
            G, Tg, n, off, C, nch = blk.G, blk.Tg, blk.n, blk.off, blk.C, blk.nch
            sfx = blk.kind
            rmask = C_("rmask_" + sfx)[:, :n]
            xm = tmp("xm", [128, 14, 128], F32, 1)
            xm4 = lambda c: xm[:, c, :n].rearrange("p (g t) -> p g t", t=Tg)
            tt(xm[:, :, :n].rearrange("p c (g t) -> p c g t", t=Tg), cT[:, :, :, c0:c0 + Tg], cT[:, :, :, c0 + 1:c0 + 1 + Tg], SUB, [cT], [xm])
            for c in range(14):
                stt(xm4(c), xm4(c), pc(L, "mu", c), cT[:, c, :, c0 + 1:c0 + 1 + Tg], MUL, ADD, [xm, prm[L], cT], [xm])
            lw = tmp("lw", [128, 128], BF16, 1); sgl = tmp("sgl", [128, 128], BF16, 1)
            act(lw[0:64, :n], xm[0:64, 12, :n], AF.Tanh, [xm], [lw])
            cp(lw[64:128, :n], xm[64:128, 12, :n], [xm], [lw])
            act(sgl[:, :n], xm[:, 13, :n], AF.Sigmoid, [xm], [sgl])
            lo = tmp("lora", [128, 3, 512], BF16, 1)
            P.dma("pool", lo[0:64, 0, :], wup_d[L], lo, writes=[lo])
            P.dma("pool", lo[64:128, 1, :], aup_d[L], lo, writes=[lo])
            P.dma("pool", lo[:, 2, :], gup_d[L], lo, writes=[lo])
            sw = tmp("sw", [128, 4, 128], F32, 1); aT = tmp("aT", [128, 4, 128], F32, 1); gT = tmp("gT", [128, 4, 128], F32, 1)
            for c in range(4):
                cs = slice(c * 128, (c + 1) * 128)
                b = bank(); mm(b, b[:, :n], lo[0:64, 0, cs], lw[0:64, :n], True, True, [lo, lw])
                act(sw[:, c, :n], b[:, :n], AF.Sigmoid, [b, prm[L]], [sw], bias=pc(L, "w0", c))
                b = bank(); mm(b, b[:, :n], lo[64:128, 1, cs], lw[64:128, :n], True, True, [lo, lw])
                act(aT[:, c, :n], b[:, :n], AF.Sigmoid, [b, prm[L]], [aT], bias=pc(L, "a0", c))
                b = bank(); mm(b, b[:, :n], lo[:, 2, cs], sgl[:, :n], True, True, [lo, sgl])
                cp(gT[:, c, :n], b[:, :n], [b], [gT])
            rT = lambda c: xm[:, c, :n]; kT = lambda c: xm[:, 4 + c, :n]; vT = lambda c: xm[:, 8 + c, :n]
            kkn = tmp("kkn", [128, 4, 128], F32, 1); kp = tmp("kp", [128, 4, 128], F32, 1); bet = tmp("bet", [128, 4, 128], F32, 1)
            t4 = tmp("t4", [128, 4, 128], F32, 1); sqb = tmp("sqb", [128, 4, 128], BF16, 1)
            for c in range(4):
                act(kkn[:, c, :n], kT(c), AF.Identity, [xm, prm[L]], [kkn], scale=pc(L, "k_k", c))
            tt(sqb[:, :, :n], kkn[:, :, :n], kkn[:, :, :n], MUL, [kkn], [sqb])
            for c in range(4):
                b = bank(); mm(b, b[:, :n], blk64b, sqb[:, c, :n], True, True, [cstb, sqb])
                act(t4[:, c, :n], b[:, :n], AF.Sqrt, [b], [t4])
            ts(t4[:, :, :n], t4[:, :, :n], 1e-12, None, MAX, None, [t4], [t4])
            act(t4[:, :, :n], t4[:, :, :n], AF.Ln, [t4], [t4])
            act(t4[:, :, :n], t4[:, :, :n], AF.Exp, [t4], [t4], scale=-1.0)
            tt(kkn[:, :, :n], kkn[:, :, :n], t4[:, :, :n], MUL, [kkn, t4], [kkn])
            for c in range(4):
                act(t4[:, c, :n], aT[:, c, :n], AF.Identity, [aT, prm[L], der[L]], [t4], scale=pc(L, "k_a", c), bias=dc(L, 20 + c))
            tt(kp[:, :, :n], xm[:, 4:8, :n], t4[:, :, :n], MUL, [xm, t4], [kp])
            tt(bet[:, :, :n], kkn[:, :, :n], aT[:, :, :n], MUL, [kkn, aT], [bet])
            bon = tmp("bon", [128, 4, 128], F32, 1)
            for c in range(4):
                act(t4[:, c, :n], rT(c), AF.Identity, [xm, prm[L]], [t4], scale=pc(L, "r_k", c))
            tt(sqb[:, :, :n], t4[:, :, :n], kp[:, :, :n], MUL, [t4, kp], [sqb])
            for c in range(4):
                b = bank(); mm(b, b[:, :n], blk64b, sqb[:, c, :n], True, True, [cstb, sqb])
                tt(bon[:, c, :n], b[:, :n], vT(c), MUL, [b, xm], [bon])
            lp = tmp("lp", [128, 4, 128], F32, 1)
            for c in range(4):
                scan(lp[:, c, :n], rmask, sw[:, c, :n], [cst, sw], [lp])
            einc = tmp("einc", [128, 4, 128], F32, 1)
            act(einc[:, :, :n], lp[:, :, :n], AF.Exp, [lp], [einc], scale=-DEC_C)
            atl = tmp("atl", [128, 4, 128], BF16, 1); btl = tmp("btl", [128, 4, 128], BF16, 1)
            ktl = tmp("ktl", [128, 4, 128], BF16, 1); rtl = tmp("rtl", [128, 4, 128], BF16, 1)
            bdc = tmp("bdc", [128, 4, 128], BF16, 1); kdc = tmp("kdc", [128, 4, 128], BF16, 1); vbf = tmp("vbf", [128, 4, 128], BF16, 1)
            tt(rtl[:, :, :n], xm[:, 0:4, :n], einc[:, :, :n], MUL, [xm, einc], [rtl])
            cp(vbf[:, :, :n], xm[:, 8:12, :n], [xm], [vbf])
            act(t4[:, :, :n], lp[:, :, :n], AF.Exp, [lp], [t4], scale=DEC_C)
            tt(btl[:, :, :n], bet[:, :, :n], t4[:, :, :n], MUL, [bet, t4], [btl])
            tt(ktl[:, :, :n], kp[:, :, :n], t4[:, :, :n], MUL, [kp, t4], [ktl])
            tt(t4[:, :, :n], lp[:, :, :n], sw[:, :, :n], SUB, [lp, sw], [t4])
            act(t4[:, :, :n], t4[:, :, :n], AF.Exp, [t4], [t4], scale=-DEC_C)
            stt(atl[:, :, :n], kkn[:, :, :n], -1.0, t4[:, :, :n], MUL, MUL, [kkn, t4], [atl])
            lv = lp[:, :, :n].rearrange("p c (k t) -> p c k t", t=C)
            tt(t4[:, :, :n].rearrange("p c (k t) -> p c k t", t=C), lv[:, :, :, C - 1:C].to_broadcast([128, 4, nch, C]), lv, SUB, [lp], [t4])
            act(t4[:, :, :n], t4[:, :, :n], AF.Exp, [t4], [t4], scale=-DEC_C)
            tt(bdc[:, :, :n], bet[:, :, :n], t4[:, :, :n], MUL, [bet, t4], [bdc])
            tt(kdc[:, :, :n], kp[:, :, :n], t4[:, :, :n], MUL, [kp, t4], [kdc])
            if CSTOP <= 1:
                return
            hp = lambda h: slice((h % 2) * 64, (h % 2) * 64 + 64)
            for (toff, nt, chunks) in blk.tiles:
                tsl = slice(toff, toff + nt)
                tok = {}
                for name, src in (("v", vbf), ("a", atl), ("bd", bdc), ("kd", kdc)):
                    for c in range(4):
                        tr(bankb, bankb[:nt, c * 128:(c + 1) * 128], src[:, c, tsl], identb, [src, cstb])
                    d = tmp("tok_" + name, [128, 512], BF16, 1)
                    cp(d[:nt, :], bankb[:nt, 0:512], [bankb], [d], eng="dve")
                    tok[name] = d
                vtk, atk, bdt, kdt = tok["v"], tok["a"], tok["bd"], tok["kd"]
                pre_m = []
                if sfx == "p":
                    for i_, (r0_, Cc_) in enumerate(chunks):
                        ind_ = C_("indp")[:nt, i_:i_ + 1]
                        bdm_ = tmp("bdm", [128, 512], BF16, 2); kdm_ = tmp("kdm", [128, 512], BF16, 2)
                        ts(bdm_[:nt, :], bdt[:nt, :], ind_, None, MUL, None, [bdt, cst], [bdm_])
                        ts(kdm_[:nt, :], kdt[:nt, :], ind_, None, MUL, None, [kdt, cst], [kdm_])
                        pre_m.append((bdm_, kdm_))
                if CSTOP <= 1.3:
                    continue
                Z = tmp("Z", [128, 8, 128], INV, 1)
                cp(Z[:nt, :, 64:128], atk[:nt, :].rearrange("p (h v) -> p h v", v=64), [atk], [Z], eng="dve")
                A4 = tmp("A4", [128, 8, 4, 128], BF16, 1)
                Nt = tmp("Nt", [128, 8, 128], INV, 1)
                mstr = C_("mstr_" + sfx)[:nt, :nt]; minc = C_("minc_" + sfx)[:nt, :nt]; mlow = C_("mlow_" + sfx)[:nt, :nt]
                for h in range(8):
                    c = h // 2; pp = hp(h)
                    b = bank()
                    mm(b, b[:nt, 0:nt], btl[pp, c, tsl], atl[pp, c, tsl], True, True, [btl, atl])
                    mm(b, b[:nt, 128:128 + nt], btl[pp, c, tsl], rtl[pp, c, tsl], True, True, [btl, rtl])
                    mm(b, b[:nt, 256:256 + nt], ktl[pp, c, tsl], atl[pp, c, tsl], True, True, [ktl, atl])
                    mm(b, b[:nt, 384:384 + nt], ktl[pp, c, tsl], rtl[pp, c, tsl], True, True, [ktl, rtl])
                    b4 = b[:nt, :].rearrange("p (k t) -> p k t", t=128)
                    tt(A4[:nt, h, 0::2, :nt], b4[:, 0::2, :nt], mstr.unsqueeze(1).to_broadcast([nt, 2, nt]), MUL, [b, cst], [A4])
                    tt(A4[:nt, h, 1::2, :nt], b4[:, 1::2, :nt], minc.unsqueeze(1).to_broadcast([nt, 2, nt]), MUL, [b, cst], [A4])
                for q in range(2):
                    b = bank()
                    for h in range(q, 8, 2):
                        c = h // 2; pp = hp(h)
                        mm(b, b[:nt, c * 128:c * 128 + nt], atl[pp, c, tsl], btl[pp, c, tsl], True, True, [atl, btl])
                    tt(Nt[:nt, q::2, :nt], b[:nt, :].rearrange("p (h t) -> p h t", t=128)[:, :, :nt],
                       mlow.unsqueeze(1).to_broadcast([nt, 4, nt]), MUL, [b, cst], [Nt])
                bx_ = bank()
                for h in range(8):
                    mm(bx_, bx_[:nt, h * 64:(h + 1) * 64], A4[:nt, h, 2, :nt], vtk[:nt, h * 64:(h + 1) * 64], True, True, [A4, vtk])
                cp(Z[:nt, :, 0:64], bx_[:nt, :].rearrange("p (h v) -> p h v", v=64), [bx_], [Z])
                if CSTOP <= 2:
                    continue
                S_ = tmp("Sinv", [128, 8, 128], INV, 1)
                tt(S_[:nt, :, :nt], A4[:nt, :, 0, :nt], identf[:nt, :nt].unsqueeze(1).to_broadcast([nt, 8, nt]), ADD, [A4, cst], [S_])
                Pm, Qm = None, Nt
                Pat = lambda h_: (A4[:nt, h_, 0, :nt] if Pm is None else Pm[:nt, h_, :nt])
                Ptile = lambda: (A4 if Pm is None else Pm)
                nlev = 5 if sfx == "p" else 1
                for lev in range(nlev):
                    Qn = tmp("Qn", [128, 8, 128], INV, 2)
                    Pn = tmp("Pn", [128, 8, 128], INV, 2) if lev < nlev - 1 else None
                    for hh in range(2):
                        b = bank()
                        for h in range(4 * hh, 4 * hh + 4):
                            mm(b, b[:nt, (h % 4) * 128:(h % 4) * 128 + nt], Pat(h), Qm[:nt, h, :nt], True, True, [Ptile(), Qm])
                        cp(Qn[:nt, 4 * hh:4 * hh + 4, :nt], b[:nt, :].rearrange("p (h t) -> p h t", t=128)[:, :, :nt], [b], [Qn])
                        if Pn is not None:
                            b = bank()
                            for h in range(4 * hh, 4 * hh + 4):
                                mm(b, b[:nt, (h % 4) * 128:(h % 4) * 128 + nt], Qm[:nt, h, :nt], Pat(h), True, True, [Ptile(), Qm])
                            cp(Pn[:nt, 4 * hh:4 * hh + 4, :nt], b[:nt, :].rearrange("p (h t) -> p h t", t=128)[:, :, :nt], [b], [Pn], eng="dve")
                    for hh in range(2):
                        b = bank()
                        for h in range(4 * hh, 4 * hh + 4):
                            mm(b, b[:nt, (h % 4) * 128:(h % 4) * 128 + nt], Qn[:nt, h, :nt], S_[:nt, h, :nt], True, True, [Qn, S_])
                        tt(S_[:nt, 4 * hh:4 * hh + 4, :nt], S_[:nt, 4 * hh:4 * hh + 4, :nt], b[:nt, :].rearrange("p (h t) -> p h t", t=128)[:, :, :nt], ADD, [S_, b], [S_])
                    Pm, Qm = Pn, Qn
                if CSTOP <= 3:
                    continue
                UW = tmp("UW", [128, 8, 128], BF16, 1)
                for hh in range(2):
                    b = bank()
                    for h in range(4 * hh, 4 * hh + 4):
                        mm(b, b[:nt, (h % 4) * 128:(h % 4) * 128 + 128], S_[:nt, h, :nt], Z[:nt, h, :], True, True, [S_, Z])
                    cp(UW[:nt, 4 * hh:4 * hh + 4, :], b[:nt, :].rearrange("p (h t) -> p h t", t=128), [b], [UW])
                b = bank()
                for h in range(8):
                    mm(b, b[hp(h), (h // 2) * 128:(h // 2) * 128 + nt], UW[:nt, h, 64:128], A4[:nt, h, 1, :nt], True, True, [UW, A4])
                Qe = tmp("Qe", [128, 4, 128], BF16, 1)
                tt(Qe[:, :, :nt], b[:, :].rearrange("p (c t) -> p c t", t=128)[:, :, :nt], rtl[:, :, tsl], ADD, [b, rtl], [Qe])
                if CSTOP <= 4:
                    continue
                bo2 = bank_o2
                if sfx == "p":
                    M, Mbz = rwM[L]
                for i, (r0, Cc) in enumerate(chunks):
                    col = toff + r0 + Cc - 1
                    ind = (C_("indp") if sfx == "p" else C_("ind"))[:nt, i:i + 1]
                    uwm = tmp("uwm", [128, 8, 128], BF16, 2)
                    ts(uwm[:nt], UW[:nt], ind, None, MUL, None, [UW, cst], [uwm])
                    if pre_m:
                        bdm, kdm = pre_m[i]
                    else:
                        bdm = tmp("bdm", [128, 512], BF16, 2); kdm = tmp("kdm", [128, 512], BF16, 2)
                        ts(bdm[:nt, :], bdt[:nt, :], ind, None, MUL, None, [bdt, cst], [bdm])
                        ts(kdm[:nt, :], kdt[:nt, :], ind, None, MUL, None, [kdt, cst], [kdm])
                    if sfx == "p":
                        Mcur, Mbzc = M, Mbz
                    else:
                        if i == 0:
                            nxtM = tmp("rMin", [128, 4, 64], F32, 3)
                            P.dma("sp", nxtM[:], rw_in[L, :, 0], nxtM, writes=[nxtM])
                        Mcur = nxtM
                        if i + 1 < len(chunks):
                            nxtM = tmp("rMin", [128, 4, 64], F32, 3)
                            P.dma("sp", nxtM[:], rw_in[L, :, i + 1], nxtM, writes=[nxtM])
                        Mbzc = tmp("rMinb", [128, 8, 64], BF16, 2)
                        memset(Mbzc[:], 0.0, [Mbzc])
                        for q in range(2):
                            cp(Mbzc[q * 64:(q + 1) * 64, q::2, :], Mcur[q * 64:(q + 1) * 64, :, :], [Mcur], [Mbzc])
                    for h in range(8):
                        c = h // 2
                        mm(bo2, bo2[hp(h), c * 128 + r0:c * 128 + r0 + Cc], Mbzc[:, h, :], Qe[:, c, r0:r0 + Cc], True, True, [Mbzc, Qe])
                    bg = bank()
                    for h in range(8):
                        mm(bg, bg[hp(h), (h // 2) * 64:(h // 2) * 64 + 64], uwm[:nt, h, 64:128], bdt[:nt, h * 64:(h + 1) * 64], True, True, [uwm, bdt])
                    GT = tmp("GT", [128, 8, 64], F32, 2)
                    memset(GT[:], 0.0, [GT])
                    for h in range(8):
                        stt(GT[hp(h), h, :], identpair[hp(h), :], einc[hp(h), h // 2, col:col + 1], bg[hp(h), (h // 2) * 64:(h // 2) * 64 + 64], MUL, ADD, [cst, einc, bg], [GT])
                    bm = bank()
                    for h in range(8):
                        o_ = bm[hp(h), (h // 2) * 64:(h // 2) * 64 + 64]
                        mm(bm, o_, bdm[:nt, h * 64:(h + 1) * 64], UW[:nt, h, 0:64], True, False, [bdm, UW])
                        mm(bm, o_, kdm[:nt, h * 64:(h + 1) * 64], vtk[:nt, h * 64:(h + 1) * 64], False, False, [kdm, vtk])
                        mm(bm, o_, GT[:, h, :], Mcur[:, h // 2, :], False, True, [GT, Mcur])
                    bm3 = bm[:, 0:256].rearrange("p (c v) -> p c v", v=64)
                    if sfx == "p":
                        M2 = ptmp("rwM%d" % L, [128, 4, 64], F32, 2); Mbz2 = ptmp("rwMbz%d" % L, [128, 8, 64], BF16, 3)
                        cp(M2[:], bm3, [bm], [M2], eng="dve")
                        memset(Mbz2[:], 0.0, [Mbz2])
                        for q in range(2):
                            cp(Mbz2[q * 64:(q + 1) * 64, q::2, :], bm[q * 64:(q + 1) * 64, 0:256].rearrange("p (c v) -> p c v", v=64), [bm], [Mbz2])
                        M, Mbz = M2, Mbz2
                    else:
                        Mout = tmp("rMout", [128, 4, 64], F32, 2)
                        cp(Mout[:], bm3, [bm], [Mout], eng="dve")
                        P.dma("sp", rw_o[L, :, 1 + i], Mout[:], Mout, reads=[Mout])
                if sfx == "p":
                    rwM[L] = (M, Mbz)
                    if last and toff + nt == n:
                        P.dma("sp", rw_o[L, :, 0], M[:], M, reads=[M])
                if CSTOP <= 5:
                    continue
                bo = bank()
                for h in range(8):
                    c = h // 2
                    o_ = bo[hp(h), c * 128:c * 128 + nt]
                    mm(bo, o_, UW[:nt, h, 0:64], A4[:nt, h, 1, :nt], True, False, [UW, A4])
                    mm(bo, o_, vtk[:nt, h * 64:(h + 1) * 64], A4[:nt, h, 3, :nt], False, True, [vtk, A4])
                bo3 = bo[:, :].rearrange("p (c t) -> p c t", t=128)[:, :, :nt]
                oS = tmp("oS", [128, 4, 128], F32, 1); dv = tmp("dv", [128, 4, 128], F32, 1)
                cp(oS[:, :, :nt], bo3, [bo], [oS])
                tt(oS[:, :, :nt], oS[:, :, :nt], bo2[:, :].rearrange("p (c t) -> p c t", t=128)[:, :, :nt], ADD, [oS, bo2], [oS])
                b1 = bank()
                for c in range(4):
                    mm(b1, b1[:, c * 128:c * 128 + nt], blkmean, oS[:, c, :nt], True, True, [cst, oS])
                tt(dv[:, :, :nt], oS[:, :, :nt], b1[:, :].rearrange("p (c t) -> p c t", t=128)[:, :, :nt], SUB, [oS, b1], [dv])
                tt(oS[:, :, :nt], dv[:, :, :nt], dv[:, :, :nt], MUL, [dv], [oS])
                b2 = bank()
                for c in range(4):
                    mm(b2, b2[:, c * 128:c * 128 + nt], blkmean, oS[:, c, :nt], True, True, [cst, oS])
                act(oS[:, :, :nt], b2[:, :].rearrange("p (c t) -> p c t", t=128)[:, :, :nt], AF.Ln, [b2], [oS], bias=GN_EPS)
                act(oS[:, :, :nt], oS[:, :, :nt], AF.Exp, [oS], [oS], scale=-0.5)
                tt(dv[:, :, :nt], dv[:, :, :nt], oS[:, :, :nt], MUL, [dv, oS], [dv])
                for c in range(4):
                    act(dv[:, c, :nt], dv[:, c, :nt], AF.Identity, [dv, prm[L]], [dv], scale=pc(L, "ln_g", c), bias=pc(L, "ln_b", c))
                tt(dv[:, :, :nt], dv[:, :, :nt], bon[:, :, tsl], ADD, [dv, bon], [dv])
                tt(brTv[2][:, :, off + toff:off + toff + nt], dv[:, :, :nt], gT[:, :, tsl], MUL, [dv, gT], [U_])

        def phaseGO(L, blks):
            mixin = tmp("mixin", [128, 8, NQ], BF16); hTall = tmp("hTall", [128, 8, NQ], BF16)
            for blk in blks:
                n, off = blk.n, blk.off
                r = rms_rstd([(xT[:, c, off:off + n], xT) for c in range(8)], n, 1024.0)
                for c in range(8):
                    stt(hTall[:, c, off:off + n], xT[:, c, off:off + n], pc(L, "g_pre", c), r[:, :n], MUL, MUL, [xT, prm[L], r], [hTall])
            for e in range(8):
                wg = [wload("wgt%d" % nn, WinT_d[L, 38 + nn * 8 + e], [128, 8, 128], 2) for nn in range(3)]
                wb = [wload("wbr%d" % nn, Wbr_d[L, e, nn], [128, 4, 128], 2) for nn in range(3)]
                for blk in blks:
                    n, off = blk.n, blk.off
                    acc = tmp("gacc", [128, 512], F32, 2)
                    for nn in range(3):
                        bg = bank()
                        for kc in range(8):
                            mm(bg, bg[:, :n], wg[nn][:, kc, :], hTall[:, kc, off:off + n], kc == 0, kc == 7, [wg[nn], hTall])
                        sg = tmp("gsg", [128, 512], F32, 2)
                        act(sg[:, :n], bg[:, :n], AF.Sigmoid, [bg], [sg])
                        bu = bank()
                        for kc in range(4):
                            mm(bu, bu[:, :n], wb[nn][:, kc, :], brTv[nn][:, kc, off:off + n], kc == 0, kc == 3, [wb[nn], U_])
                        if nn == 0:
                            tt(acc[:, :n], sg[:, :n], bu[:, :n], MUL, [sg, bu], [acc])
                        else:
                            tt(sg[:, :n], sg[:, :n], bu[:, :n], MUL, [sg, bu], [sg])
                            if nn == 1:
                                tt(acc[:, :n], acc[:, :n], sg[:, :n], ADD, [acc, sg], [acc])
                            else:
                                tt(mixin[:, e, off:off + n], acc[:, :n], sg[:, :n], ADD, [acc, sg], [mixin])
            wo = tmp("wo", [128, 8, 1024], BF16)
            for kc in range(8):
                P.dma("pool", wo[:, kc, :], Wout_d[L, :, kc, :], wo, writes=[wo])
            for blk in blks:
                n, off = blk.n, blk.off
                stage = tmp("stage", [128, 8, 512], F32, 1)
                for e in range(8):
                    b = bank()
                    for kc in range(8):
                        mm(b, b[:, :n], wo[:, kc, e * 128:(e + 1) * 128], mixin[:, kc, off:off + n], kc == 0, kc == 7, [wo, mixin])
                    cp(stage[:, e, :n], b[:, :n], [b], [stage])
                postnorm_residual(L, "g_postmix", blk, stage)

        def phaseF(L, blks):
            stageF = tmp("stageF", [128, 8, NQ], F32); h2 = tmp("h2", [128, 8, NQ], BF16)
            for blk in blks:
                n, off = blk.n, blk.off
                r = rms_rstd([(xT[:, c, off:off + n], xT) for c in range(8)], n, 1024.0)
                for c in range(8):
                    stt(h2[:, c, off:off + n], xT[:, c, off:off + n], pc(L, "g_preffn", c), r[:, :n], MUL, MUL, [xT, prm[L], r], [h2])
            NQa = NQ
            actv = lambda k: U_[:, 0:11 * NQa].rearrange("p (a b) -> p a b", b=NQa)[k]
            for half in range(2):
                for f in range(11):
                    wg = wload("wfg", Wg_d[L, half * 11 + f], [128, 8, 128], 5)
                    wu = wload("wfu", Wu_d[L, half * 11 + f], [128, 8, 128], 5)
                    for blk in blks:
                        n, off = blk.n, blk.off
                        bg = bank(); bu = bank()
                        for kc in range(8):
                            mm(bg, bg[:, :n], wg[:, kc, :], h2[:, kc, off:off + n], kc == 0, kc == 7, [wg, h2])
                        for kc in range(8):
                            mm(bu, bu[:, :n], wu[:, kc, :], h2[:, kc, off:off + n], kc == 0, kc == 7, [wu, h2])
                        sg = tmp("fsg", [128, 512], F32, 2)
                        act(sg[:, :n], bg[:, :n], AF.Silu, [bg], [sg])
                        tt(actv((slice(None), f, slice(off, off + n))), sg[:, :n], bu[:, :n], MUL, [sg, bu], [U_])
                for e in range(8):
                    wd = wload("wfd", Wd_d[L, e, half], [128, 11, 128], 4)
                    for blk in blks:
                        n, off = blk.n, blk.off
                        b = bank()
                        for kc in range(11):
                            mm(b, b[:, :n], wd[:, kc, :], actv((slice(None), kc, slice(off, off + n))), kc == 0, kc == 10, [wd, U_])
                        if half == 0:
                            cp(stageF[:, e, off:off + n], b[:, :n], [b], [stageF])
                        else:
                            tt(stageF[:, e, off:off + n], stageF[:, e, off:off + n], b[:, :n], ADD, [stageF, b], [stageF])
            for blk in blks:
                postnorm_residual(L, "g_postffn", blk, stageF, blk.off)

        def phaseP(L, blks):
            tmp("ppad", [128, 8704 * 2], BF16)
            wpl = tmp("wpl", [128, 2, 1024], BF16); wpg = tmp("wpg", [128, 8, 1024], BF16)
            for kc in range(2):
                P.dma("pool", wpl[:, kc, :], Wple_d[L, :, kc, :], wpl, writes=[wpl])
            for kc in range(8):
                P.dma("pool", wpg[:, kc, :], Wpg_d[L, :, kc, :], wpg, writes=[wpg])
            for blk in blks:
                n, off = blk.n, blk.off
                xb = tmp("xb", [128, 8, 512], BF16, 1); pb = tmp("pb", [128, 2, 512], BF16, 2)
                cp(xb[:, :, :n], xT[:, :, off:off + n], [xT], [xb], eng="dve")
                for kc in range(2):
                    P.dma("pool", pb[:, kc, :n], pT_d[L, :, kc, blk.g0:blk.g0 + n], pb, writes=[pb])
                stage = tmp("stage", [128, 8, 512], F32, 1)
                for e in range(8):
                    es = slice(e * 128, (e + 1) * 128)
                    b1 = bank(); b2 = bank()
                    for kc in range(2):
                        mm(b1, b1[:, :n], wpl[:, kc, es], pb[:, kc, :n], kc == 0, kc == 1, [wpl, pb])
                    for kc in range(8):
                        mm(b2, b2[:, :n], wpg[:, kc, es], xb[:, kc, :n], kc == 0, kc == 7, [wpg, xb])
                    sg = tmp("psg", [128, 512], F32, 2)
                    act(sg[:, :n], b2[:, :n], AF.Sigmoid, [b2], [sg])
                    tt(stage[:, e, :n], sg[:, :n], b1[:, :n], MUL, [sg, b1], [stage])
                postnorm_residual(L, "g_ple", blk, stage)

        for H in range(2):
            g_base = H * 1024
            b512 = [Blk("p", 0, 512, g_base), Blk("p", 512, 512, g_base + 512)]
            b256 = [Blk("p", i * 256, 256, g_base + i * 256) for i in range(4)]
            if H == 1:
                b512.append(Blk("s", 1024, 64, 2048)); b256.append(Blk("s", 1024, 64, 2048))
            for blk in b512:
                for c in range(8):
                    P.dma("sp", xT[:, c, blk.off:blk.off + blk.n], xT_d[:, c, blk.g0:blk.g0 + blk.n], xT, writes=[xT])
            for L in range(NLAYERS):
                for bi_, blk in enumerate(b256):
                    lastp = (H == 1 and blk.kind == "p" and blk.off + blk.n == 1024)
                    arena_reset()
                    P.ph = "N"
                    hT = norm_to_bf16(L, "g_pre", blk)
                    if "A" in PHASES:
                        P.ph = "A"
                        phaseA(L, blk, hT, lastp)
                    if "B" in PHASES:
                        if blk.kind != "p":
                            arena_reset()
                        P.ph = "B"
                        phaseB(L, blk, hT, lastp)
                    if "C" in PHASES:
                        arena_reset()
                        if blk.kind == "p":
                            P.ph = "C"
                            cT_ = phaseC_proj(L, blk, hT, lastp)
                            for j in range(2):
                                sub = Blk("p", blk.off + j * 128, 128, blk.g0 + j * 128)
                                phaseC(L, sub, cT_, j * 128, lastp and j == 1)
                        else:
                            P.ph = "Cs"
                            cT_ = phaseC_proj(L, blk, hT, False)
                            phaseC(L, blk, cT_, 0, False)
                if DBG and H == 1 and L == 0:
                    arena_reset()
                    for i_ in range(3):
                        dt_ = tmp("dbgt", [128, 4, 64], F32, 3)
                        cp(dt_[:], brTv[i_][:, :, 1024:1088], [U_], [dt_], eng="dve")
                        P.dma("sp", dbg_d[i_], dt_[:], dt_, reads=[dt_])
                if "G" in PHASES:
                    arena_reset(); P.ph = "G"; phaseGO(L, b512)
                if "F" in PHASES:
                    arena_reset(); P.ph = "F"; phaseF(L, b512)
                if "P" in PHASES:
                    arena_reset(); P.ph = "P"; phaseP(L, b512)
            arena_reset()
            for blk in b512:
                for c in range(8):
                    P.dma("sp", yT_d[:, c, blk.g0:blk.g0 + blk.n], xT[:, c, blk.off:blk.off + blk.n], xT, reads=[xT])
        P.emit()
        build_program.stats = P.stats
        build_program.pe_ph = [o.get('ph', '') for o in P.ops if o.get('eng') == 'pe']
    return nc


_NC_CACHE = {}


def _cols(a):
    a = np.asarray(a, np.float32).reshape(-1, 128)
    return np.ascontiguousarray(a.T)


def kernel(**inp):
    f = lambda k: np.asarray(inp[k], np.float32)
    ca = np.ascontiguousarray
    prm = np.zeros((2, 128, NPRM), np.float32)
    def putp(L, name, arr):
        prm[L][:, PRM_OFF[name]:PRM_OFF[name] + arr.shape[1]] = arr
    for L in range(2):
        putp(L, "g_pre", _cols(f("norm_pre_mix")[L])); putp(L, "g_postmix", _cols(f("norm_post_mix")[L]))
        putp(L, "g_preffn", _cols(f("norm_pre_ffn")[L])); putp(L, "g_postffn", _cols(f("norm_post_ffn")[L]))
        putp(L, "g_ple", _cols(f("norm_ple")[L]))
        cw = f("conv_w")[L]
        putp(L, "conv_w", ca(cw.reshape(4, 4, 128).transpose(2, 1, 0).reshape(128, 16)))
        putp(L, "conv_b", _cols(f("conv_b")[L])); putp(L, "ba", _cols(f("lru_ba")[L])); putp(L, "bx", _cols(f("lru_bx")[L]))
        putp(L, "lam", _cols(f("lru_lambda")[L]))
        putp(L, "hlb", np.concatenate([_cols(f("hg_lower_bounds")[0]), _cols(f("hg_lower_bounds")[1])], axis=1))
        putp(L, "hg_norm_g", _cols(f("hg_norm_g")[L])); putp(L, "mu", _cols(f("rw_mu")[L])); putp(L, "w0", _cols(f("rw_w0")[L]))
        putp(L, "a0", _cols(f("rw_a0")[L])); putp(L, "k_k", _cols(f("rw_k_k")[L])); putp(L, "k_a", _cols(f("rw_k_a")[L]))
        putp(L, "r_k", _cols(f("rw_r_k")[L].reshape(-1))); putp(L, "ln_g", _cols(f("rw_ln_g")[L])); putp(L, "ln_b", _cols(f("rw_ln_b")[L]))
    w_in = f("w_in")
    WinT = ca(w_in.reshape(2, 8, 128, 62, 128).transpose(0, 3, 2, 1, 4))
    Wbi = ca(w_in[:, :, 2048:2560].reshape(2, 8, 128, 512).transpose(0, 2, 1, 3))
    def bd(w):
        o = np.zeros((2, 4, 128, 128), np.float32)
        for L in range(2):
            for c in range(4):
                o[L, c, 0:64, 0:64] = w[L, 2 * c]; o[L, c, 64:128, 64:128] = w[L, 2 * c + 1]
        return o
    shared = dict(
        prm=prm, cst=_make_cst(), WinT=WinT, Wbi=Wbi, WA=bd(f("lru_wa")), WX=bd(f("lru_wx")),
        wup=ca(f("rw_w_up")), aup=ca(f("rw_a_up")), gup=ca(f("rw_g_up")), rows=np.zeros((2, 2, 512), np.float32),
        Wbr=ca(f("w_branch").reshape(2, 3, 4, 128, 8, 128).transpose(0, 4, 1, 3, 2, 5)),
        Wout=ca(f("w_out").reshape(2, 8, 128, 1024).transpose(0, 2, 1, 3)),
        Wg=ca(f("w_ffn_gate").reshape(2, 8, 128, 22, 128).transpose(0, 3, 2, 1, 4)),
        Wu=ca(f("w_ffn_up").reshape(2, 8, 128, 22, 128).transpose(0, 3, 2, 1, 4)),
        Wd=ca(f("w_ffn_down").reshape(2, 2, 11, 128, 8, 128).transpose(0, 4, 1, 3, 2, 5)),
        Wple=ca(f("w_ple").reshape(2, 2, 128, 1024).transpose(0, 2, 1, 3)),
        Wpg=ca(f("w_ple_gate").reshape(2, 8, 128, 1024).transpose(0, 2, 1, 3)),
    )
    xp, xs, pp, psm = f("x_prompt"), f("x_sample"), f("p_prompt"), f("p_sample")
    sc, sl, sh_, sr, ss = f("state_conv_a"), f("state_lru_a"), f("state_hgrn"), f("state_rwkv"), f("state_shift_c")
    in_maps = []
    for c in range(8):
        n0 = 16 * c
        xt = np.concatenate([xp[c].T, xs[n0:n0 + 16].reshape(64, 1024).T], axis=1)
        pt = np.stack([np.concatenate([pp[L, c].T, psm[L, n0:n0 + 16].reshape(64, 256).T], axis=1) for L in range(2)])
        m = dict(shared)
        m.update(
            xT=ca(xt.reshape(8, 128, 2112).transpose(1, 0, 2)),
            pT=ca(pt.reshape(2, 2, 128, 2112).transpose(0, 2, 1, 3)),
            conv_in=ca(sc[:, n0:n0 + 16].reshape(2, 16, 3, 4, 128).transpose(0, 4, 3, 1, 2)),
            lru_in=ca(sl[:, n0:n0 + 16].reshape(2, 16, 4, 128).transpose(0, 3, 2, 1)),
            hg_in=ca(sh_[:, n0:n0 + 16]),
            rw_in=ca(sr[:, n0:n0 + 16].reshape(2, 16, 4, 2, 64, 64).transpose(0, 3, 5, 1, 2, 4).reshape(2, 128, 16, 4, 64)),
            sh_in=ca(ss[:, n0:n0 + 16].reshape(2, 16, 14, 128).transpose(0, 3, 2, 1)),
        )
        in_maps.append(m)
    if "nc" not in _NC_CACHE:
        _NC_CACHE["nc"] = build_program()
    nc = _NC_CACHE["nc"]
    res = run_bass_kernel_spmd(nc, in_maps, core_ids=list(range(8)))
    R = res.results
    y_p = np.zeros((8, 2048, 1024), np.float32); y_s = np.zeros((128, 4, 1024), np.float32)
    conv_p = np.zeros((2, 8, 3, 512), np.float32); conv_s = np.zeros((2, 128, 3, 512), np.float32)
    lru_p = np.zeros((2, 8, 512), np.float32); lru_s = np.zeros((2, 128, 512), np.float32)
    hg_p = np.zeros((2, 8, 4, 128, 128), np.float32); hg_s = np.zeros((2, 128, 4, 128, 128), np.float32)
    rw_p = np.zeros((2, 8, 8, 64, 64), np.float32); rw_s = np.zeros((2, 128, 8, 64, 64), np.float32)
    sh_p = np.zeros((2, 8, 1792), np.float32); sh_s = np.zeros((2, 128, 1792), np.float32)
    for c in range(8):
        r = R[c]; n0 = 16 * c
        y = np.asarray(r["yT"]).transpose(1, 0, 2).reshape(1024, 2112).T
        y_p[c] = y[:2048]; y_s[n0:n0 + 16] = y[2048:].reshape(16, 4, 1024)
        co = np.asarray(r["conv_o"]).transpose(0, 3, 4, 2, 1).reshape(2, 17, 3, 512)
        conv_p[:, c] = co[:, 0]; conv_s[:, n0:n0 + 16] = co[:, 1:]
        lo = np.asarray(r["lru_o"]).transpose(0, 3, 2, 1).reshape(2, 17, 512)
        lru_p[:, c] = lo[:, 0]; lru_s[:, n0:n0 + 16] = lo[:, 1:]
        ho = np.asarray(r["hg_o"])
        hg_p[:, c] = ho[:, 0]; hg_s[:, n0:n0 + 16] = ho[:, 1:]
        ro = np.asarray(r["rw_o"]).reshape(2, 2, 64, 17, 4, 64).transpose(0, 3, 4, 1, 5, 2).reshape(2, 17, 8, 64, 64)
        rw_p[:, c] = ro[:, 0]; rw_s[:, n0:n0 + 16] = ro[:, 1:]
        so = np.asarray(r["sh_o"]).transpose(0, 3, 2, 1).reshape(2, 17, 1792)
        sh_p[:, c] = so[:, 0]; sh_s[:, n0:n0 + 16] = so[:, 1:]
    if DBG:
        kernel.dbg = [np.asarray(R[c]["dbg_br"]) for c in range(8)]
    return (y_p, y_s, conv_p, lru_p, hg_p, rw_p, sh_p, conv_s, lru_s, hg_s, rw_s, sh_s)
```
